# Optimizing a Trainium2 kernel written in Bass

```python
import math
import jax, jax.numpy as jnp
from jax import lax
import numpy as np

D_MODEL = 1024
BATCH = 4
SEQ = 4096
DEPTH = 1

GMLP_CHUNK = 128
GMLP_GROUPS = 8
GMLP_WIDTH = D_MODEL
GMLP_GROUP_DIM = GMLP_WIDTH // GMLP_GROUPS
MOBA_HEADS = 8
MOBA_HEAD_DIM = 128
MOBA_WIDTH = MOBA_HEADS * MOBA_HEAD_DIM
MOBA_BLOCK = 256
MOBA_TOPK = 3
MOBA_Q_CHUNK = 32
FFN_HIDDEN = int(math.ceil(8 * D_MODEL / 3 / 256) * 256)
IN_SPLITS = (GMLP_WIDTH, GMLP_WIDTH, MOBA_WIDTH, MOBA_WIDTH, MOBA_WIDTH, D_MODEL, D_MODEL)
IN_WIDTH = sum(IN_SPLITS)
N_MOD = 6
EPS = 1e-6
NEG = -1e30

kernel_name = "hybrid_gmlp_moba_gated_block"


def _rmsnorm(x, g):
    xf = x.astype(jnp.float32)
    y = xf * lax.rsqrt(jnp.mean(xf * xf, axis=-1, keepdims=True) + EPS)
    return (y * g.astype(jnp.float32)).astype(x.dtype)


def _layernorm(x, g, b):
    xf = x.astype(jnp.float32)
    mu = jnp.mean(xf, axis=-1, keepdims=True)
    var = jnp.mean(jnp.square(xf - mu), axis=-1, keepdims=True)
    y = (xf - mu) * lax.rsqrt(var + EPS)
    return (y * g.astype(jnp.float32) + b.astype(jnp.float32)).astype(x.dtype)


def _modulate(h, shift, scale):
    return h * (1 + scale[:, None, :]) + shift[:, None, :]


def _spatial_gating(u, v, ln_g, ln_b, w_s, b_s):
    B_, S_, _ = v.shape
    v = _layernorm(v, ln_g, ln_b)
    nc = S_ // GMLP_CHUNK
    vg = v.reshape(B_, nc, GMLP_CHUNK, GMLP_GROUPS, GMLP_GROUP_DIM)
    causal = jnp.tril(jnp.ones((GMLP_CHUNK, GMLP_CHUNK), dtype=bool))
    w = w_s * causal.astype(w_s.dtype)[None]
    mixed = jnp.einsum('gts,bcsgd->bctgd', w, vg) + b_s.T[None, None, :, :, None]
    return u * mixed.reshape(B_, S_, GMLP_WIDTH)


def _moba_attention(q, k, v):
    B_, S_, H_, hd = q.shape
    bh = B_ * H_
    nb = -(-S_ // MOBA_BLOCK)
    pad = nb * MOBA_BLOCK - S_
    qh = q.transpose(0, 2, 1, 3).reshape(bh, S_, hd)
    kh = jnp.pad(k.transpose(0, 2, 1, 3).reshape(bh, S_, hd), ((0, 0), (0, pad), (0, 0)))
    vh = jnp.pad(v.transpose(0, 2, 1, 3).reshape(bh, S_, hd), ((0, 0), (0, pad), (0, 0)))
    kb = kh.reshape(bh, nb, MOBA_BLOCK, hd)
    vb = vh.reshape(bh, nb, MOBA_BLOCK, hd)
    kbar = jnp.mean(kb.astype(jnp.float32), axis=2)
    topk = min(MOBA_TOPK, nb)
    scale = hd ** -0.5
    blk_ids = jnp.arange(nb)
    gather_blocks = jax.vmap(lambda blocks, idx: blocks[idx])

    def chunk(ci):
        start = ci * MOBA_Q_CHUNK
        qc = lax.dynamic_slice_in_dim(qh, start, MOBA_Q_CHUNK, axis=1)
        t = start + jnp.arange(MOBA_Q_CHUNK)
        qblk = start // MOBA_BLOCK
        gate = jnp.einsum('nqd,nbd->nqb', qc.astype(jnp.float32), kbar)
        gate = jnp.where((blk_ids < qblk)[None, None, :], gate, NEG)
        _, idx = lax.top_k(gate, topk)
        valid = idx < qblk
        k_sel = gather_blocks(kb, idx)
        v_sel = gather_blocks(vb, idx)
        s_sel = jnp.einsum('nqd,nqjkd->nqjk', qc, k_sel).astype(jnp.float32) * scale
        s_sel = jnp.where(valid[..., None], s_sel, NEG).reshape(bh, MOBA_Q_CHUNK, topk * MOBA_BLOCK)
        k_own = lax.dynamic_slice_in_dim(kh, qblk * MOBA_BLOCK, MOBA_BLOCK, axis=1)
        v_own = lax.dynamic_slice_in_dim(vh, qblk * MOBA_BLOCK, MOBA_BLOCK, axis=1)
        s_own = jnp.einsum('nqd,nkd->nqk', qc, k_own).astype(jnp.float32) * scale
        kpos = qblk * MOBA_BLOCK + jnp.arange(MOBA_BLOCK)
        s_own = jnp.where((kpos[None, :] <= t[:, None])[None], s_own, NEG)
        p = jax.nn.softmax(jnp.concatenate([s_sel, s_own], axis=-1), axis=-1)
        p_sel = p[..., :topk * MOBA_BLOCK].reshape(bh, MOBA_Q_CHUNK, topk, MOBA_BLOCK).astype(v.dtype)
        p_own = p[..., topk * MOBA_BLOCK:].astype(v.dtype)
        return (jnp.einsum('nqjk,nqjkd->nqd', p_sel, v_sel)
                + jnp.einsum('nqk,nkd->nqd', p_own, v_own))

    out = lax.map(chunk, jnp.arange(S_ // MOBA_Q_CHUNK))
    out = out.transpose(1, 0, 2, 3).reshape(B_, H_, S_, hd).transpose(0, 2, 1, 3)
    return out.reshape(B_, S_, H_ * hd)


def setup_inputs(seed: int = 0) -> dict:
    key = jax.random.key(seed)
    ks = jax.random.split(key, 20)
    f32 = jnp.float32
    L, D = DEPTH, D_MODEL
    nrm = lambda k, shape, s: (jax.random.normal(k, shape, f32) * s)
    return {
        "x": nrm(ks[0], (BATCH, SEQ, D), 1.0),
        "c": nrm(ks[1], (BATCH, D), 1.0),
        "w_ada": nrm(ks[2], (L, D, N_MOD * D), 0.5 * D ** -0.5),
        "b_ada": nrm(ks[3], (L, N_MOD * D), 0.01),
        "norm_mix_g": 1.0 + nrm(ks[4], (L, D), 0.02),
        "w_in": nrm(ks[5], (L, D, IN_WIDTH), D ** -0.5),
        "ln_v_g": 1.0 + nrm(ks[6], (L, GMLP_WIDTH), 0.02),
        "ln_v_b": nrm(ks[7], (L, GMLP_WIDTH), 0.01),
        "w_spatial": nrm(ks[8], (L, GMLP_GROUPS, GMLP_CHUNK, GMLP_CHUNK), GMLP_CHUNK ** -0.5),
        "b_spatial": 1.0 + nrm(ks[9], (L, GMLP_GROUPS, GMLP_CHUNK), 0.01),
        "w_proj_a": nrm(ks[10], (L, GMLP_WIDTH, D), GMLP_WIDTH ** -0.5),
        "w_proj_b": nrm(ks[11], (L, MOBA_WIDTH, D), MOBA_WIDTH ** -0.5),
        "w_out": nrm(ks[12], (L, D, D), D ** -0.5),
        "norm_ffn_g": 1.0 + nrm(ks[13], (L, D), 0.02),
        "w_ffn_gate": nrm(ks[14], (L, D, FFN_HIDDEN), D ** -0.5),
        "w_ffn_up": nrm(ks[15], (L, D, FFN_HIDDEN), D ** -0.5),
        "w_ffn_down": nrm(ks[16], (L, FFN_HIDDEN, D), FFN_HIDDEN ** -0.5),
        "norm_final_g": 1.0 + nrm(ks[17], (D,), 0.02),
    }


def reference(x, c, w_ada, b_ada, norm_mix_g, w_in, ln_v_g, ln_v_b, w_spatial, b_spatial,
              w_proj_a, w_proj_b, w_out, norm_ffn_g, w_ffn_gate, w_ffn_up, w_ffn_down,
              norm_final_g):
    B_, S_, D = x.shape
    split_pts = list(np.cumsum(IN_SPLITS)[:-1])
    c_act = jax.nn.silu(c)
    for l in range(DEPTH):
        mod = c_act @ w_ada[l] + b_ada[l]
        sh_m, sc_m, g_m, sh_f, sc_f, g_f = jnp.split(mod, N_MOD, axis=-1)

        h = _modulate(_rmsnorm(x, norm_mix_g[l]), sh_m, sc_m)
        proj = h @ w_in[l]
        uv_u, uv_v, q, k, v, gate_a, gate_b = jnp.split(proj, split_pts, axis=-1)
        y_a = _spatial_gating(jax.nn.gelu(uv_u), jax.nn.gelu(uv_v),
                              ln_v_g[l], ln_v_b[l], w_spatial[l], b_spatial[l])
        y_b = _moba_attention(q.reshape(B_, S_, MOBA_HEADS, MOBA_HEAD_DIM),
                              k.reshape(B_, S_, MOBA_HEADS, MOBA_HEAD_DIM),
                              v.reshape(B_, S_, MOBA_HEADS, MOBA_HEAD_DIM))
        merged = (jax.nn.sigmoid(gate_a) * (y_a @ w_proj_a[l])
                  + jax.nn.sigmoid(gate_b) * (y_b @ w_proj_b[l]))
        x = x + g_m[:, None, :] * (merged @ w_out[l])

        h = _modulate(_rmsnorm(x, norm_ffn_g[l]), sh_f, sc_f)
        ff = (jax.nn.silu(h @ w_ffn_gate[l]) * (h @ w_ffn_up[l])) @ w_ffn_down[l]
        x = x + g_f[:, None, :] * ff
    return _rmsnorm(x, norm_final_g)
```

```python
import contextlib
import numpy as np
import concourse.bass as bass
import concourse.mybir as mybir
from concourse.bass_utils import run_bass_kernel_spmd

F32 = mybir.dt.float32
BF16 = mybir.dt.bfloat16
AF = mybir.ActivationFunctionType
ALU = mybir.AluOpType

D = 1024
NT = 2048
NTILE = 16
HD = 128
NH = 8
FH = 2816
NFC = 22
EPS = 1e-6
NEG = -1e30
QSCALE = float(HD ** -0.5)


class Res:
    __slots__ = ("name", "w", "r", "dsem", "dcnt")

    def __init__(self, name):
        self.name = name
        self.w = None
        self.r = {}
        self.dsem = None
        self.dcnt = 0


class Prog:
    ENGS = ("pe", "act", "dve", "pool", "sp")

    def __init__(self, nc, stack):
        self.nc = nc
        self.stack = stack
        self.ops = {e: [] for e in self.ENGS}
        self.sem = {e: stack.enter_context(nc.semaphore("s_" + e)) for e in self.ENGS}
        self.cnt = {e: 0 for e in self.ENGS}
        self.seen = {e: {} for e in self.ENGS}
        self.nres = 0
        self.dma_toks = {}
        self.free_dsems = []

    def res(self, name=None, after=None):
        self.nres += 1
        r = Res(f"{name or 'r'}_{self.nres}")
        if after:
            r.r = dict(after)
        return r

    def inherit(self, *olds):
        f = {}
        for i, o in enumerate(olds):
            if o.w is not None:
                f[f"I{i}w"] = o.w
            for k, t in o.r.items():
                f[f"I{i}{k}"] = t
        return f

    def fence(self):
        f = {}
        for e in ("pe", "act", "dve", "pool"):
            if self.cnt[e] > 0:
                f["F" + e] = (e, self.sem[e], self.cnt[e])
        for k, t in self.dma_toks.items():
            f["F" + k] = t
        return f

    def _need(self, eng, tok, waits, raw):
        if tok is None:
            return
        key, sem, val = tok
        if key == eng and (eng == "pe" or not raw or val <= self.cnt[eng] - 3):
            return
        if self.seen[eng].get(key, 0) >= val:
            return
        self.seen[eng][key] = val
        waits.append((sem, val))

    def _deps(self, eng, reads, writes):
        waits = []
        for r in reads:
            self._need(eng, r.w, waits, True)
        for r in writes:
            self._need(eng, r.w, waits, False)
            for t in r.r.values():
                self._need(eng, t, waits, False)
        return waits

    def op(self, eng, fn, reads=(), writes=(), inc=True):
        waits = self._deps(eng, reads, writes)
        n = self.cnt[eng] + 1
        if inc:
            self.cnt[eng] = n
        tok = (eng, self.sem[eng], n)
        for r in reads:
            r.r[eng] = tok
        for r in writes:
            r.w = tok
            r.r = {}
        sem = self.sem[eng]

        def emit(e, fn=fn, waits=waits, inc=inc, sem=sem):
            for (s, v) in waits[:-1]:
                e.wait_ge(s, v)
            ins = fn(e)
            if waits:
                ins._wait_ge(waits[-1][0], waits[-1][1])
            if inc:
                ins.then_inc(sem, 1)
        self.ops[eng].append(emit)
        return tok

    def dma(self, eng, out, in_, reads=(), writes=(), **kw):
        waits = self._deps(eng, reads, writes)
        owner = (list(writes) + list(reads))[0]
        if owner.dsem is None:
            owner.dsem = self.stack.enter_context(self.nc.semaphore("d_" + owner.name))
        owner.dcnt += 16
        key = "d_" + owner.name
        tok = (key, owner.dsem, owner.dcnt)
        self.dma_toks[key] = tok
        for r in reads:
            r.r[key] = tok
        for r in writes:
            r.w = tok
            r.r = {}
        sem = owner.dsem

        def emit(e, waits=waits, sem=sem, out=out, in_=in_, kw=kw):
            for (s, v) in waits[:-1]:
                e.wait_ge(s, v)
            ins = e.dma_start(out=out, in_=in_, **kw)
            if waits:
                ins._wait_ge(waits[-1][0], waits[-1][1])
            ins.then_inc(sem, 16)
        self.ops[eng].append(emit)
        return tok

    def wait(self, eng, tok):
        waits = []
        self._need(eng, tok, waits, True)
        if waits:
            def emit(e, waits=waits):
                for (s, v) in waits:
                    e.wait_ge(s, v)
            self.ops[eng].append(emit)

    def emit_all(self):
        nc = self.nc
        with nc.Block() as block:
            @block.tensor
            def _(e):
                for f in self.ops["pe"]:
                    f(e)

            @block.scalar
            def _(e):
                for f in self.ops["act"]:
                    f(e)

            @block.vector
            def _(e):
                for f in self.ops["dve"]:
                    f(e)

            @block.gpsimd
            def _(e):
                for f in self.ops["pool"]:
                    f(e)

            @block.sync
            def _(e):
                for f in self.ops["sp"]:
                    f(e)


def build_program(upto=99, debug=False):
    nc = bass.Bass("TRN2", target_bir_lowering=False)
    dr = lambda name, shape, dt=F32, kind="ExternalInput": nc.dram_tensor(name, shape, dt, kind=kind).ap()
    x_own = dr("x_own", [NT, D])
    x_ctx = dr("x_ctx", [NT, D])
    c_row = dr("c_row", [D])
    w_ada = dr("w_ada", [D + 1, 6 * D])[0:D, :]
    b_ada = dr("b_ada", [6 * D])
    norm_mix_g = dr("norm_mix_g", [D])
    w_in = dr("w_in", [D + 1, 7 * D])[0:D, :]
    ln_v_g = dr("ln_v_g", [D])
    ln_v_b = dr("ln_v_b", [D])
    w_spatial = dr("w_spatial", [8, 128, 128])
    b_spatial = dr("b_spatial", [8, 128])
    w_proj_a = dr("w_proj_a", [D + 1, D])[0:D, :]
    w_proj_b = dr("w_proj_b", [D + 1, D])[0:D, :]
    w_out = dr("w_out", [D + 1, D])[0:D, :]
    norm_ffn_g = dr("norm_ffn_g", [D])
    w_ffn_gate = dr("w_ffn_gate", [D + 1, FH])[0:D, :]
    w_ffn_up = dr("w_ffn_up", [D + 1, FH])[0:D, :]
    w_ffn_down = dr("w_ffn_down", [FH + 1, D])[0:FH, :]
    norm_final_g = dr("norm_final_g", [D])
    gmask_d = dr("gmask", [128, 256])
    vmask_d = dr("vmask", [128, 256])
    out_d = dr("out", [NT, D], F32, "ExternalOutput")
    dbg = {}

    st = contextlib.ExitStack()
    with st:
        P = Prog(nc, st)
        sbt = lambda name, shape, dt: st.enter_context(nc.sbuf_tensor(name, shape, dt))

        ident = sbt("ident", [128, 128], BF16); r_ident = P.res("ident")
        tri = sbt("tri", [128, 128], BF16); r_tri = P.res("tri")
        onesb = sbt("onesb", [128, 128], BF16); r_onesb = P.res("onesb")
        onesf = sbt("onesf", [128, 128], F32); r_onesf = P.res("onesf")
        cf = sbt("cf", [128, 128], F32); r_cf = P.res("cf")
        identf = sbt("identf", [128, 128], F32); r_identf = P.res("identf")
        st4 = sbt("st4", [128, 64], F32); r_st4 = P.res("st4")
        gmask = sbt("gmask_s", [128, 16, 16], F32); r_gmask = P.res("gmask")
        vmask = sbt("vmask_s", [128, 16, 16], F32); r_vmask = P.res("vmask")
        cT = sbt("cT", [128, 8], F32); r_cT = P.res("cT")
        cact = sbt("cact", [128, 8], F32); r_cact = P.res("cact")
        modT = sbt("modT", [128, 48], F32); r_modT = P.res("modT")
        gvec = sbt("gvec", [128, 4, 8], F32); r_gvec = P.res("gvec")
        AB = sbt("AB", [128, 4, 8], F32); r_AB = P.res("AB")
        Ct = sbt("Ct", [128, 8, 128], F32); r_Ct = P.res("Ct")
        wsT = sbt("wsT", [128, 8, 128], BF16); r_wsT = P.res("wsT")
        ss = sbt("ss", [128, 64], F32); r_ss = P.res("ss")
        sq = sbt("sq", [128, 64], F32); r_sq = P.res("sq")
        junk = sbt("junk", [128, 1024], BF16); r_junk = P.res("junk")
        junk2 = sbt("junk2", [128, 1024], BF16); r_junk2 = P.res("junk2")

        ARENA_KB = 192
        arena = sbt("arena", [128, ARENA_KB * 512], BF16)

        def view(off_kb, shape, dt):
            n = int(np.prod(shape[1:]))
            nb = n * (4 if dt == F32 else 2)
            o = int(round(off_kb * 512))
            assert o * 2 == int(round(off_kb * 1024)), off_kb
            assert o * 2 + nb <= ARENA_KB * 1024, (off_kb, shape)
            a = arena[:, o:o + nb // 2]
            if dt == F32:
                a = a.bitcast(F32)
            if len(shape) == 3:
                a = a.rearrange("p (a b) -> p a b", a=shape[1])
            elif len(shape) == 4:
                a = a.rearrange("p (a b c) -> p a b c", a=shape[1], b=shape[2])
            return a

        pp = [st.enter_context(nc.psum_tensor(f"pp{i}", [128, 2, 512], F32)) for i in range(4)]
        r_bank = [P.res(f"bank{i}") for i in range(8)]
        bank_ap = [pp[i // 2][:, i % 2, :] for i in range(8)]
        bank_bf = [pp[i // 2][:, i % 2, :].bitcast(BF16) for i in range(8)]
        rot = {"b": 0, "p": 0, "o": 0, "nb": 6, "np": 3, "no": 2}

        def set_rot(nb, np_, no):
            rot["nb"], rot["np"], rot["no"] = nb, np_, no
            rot["b"] %= nb; rot["p"] %= np_; rot["o"] %= no

        def next_bank():
            i = rot["b"]; rot["b"] = (i + 1) % rot["nb"]
            return i

        def next_pair():
            i = rot["p"]; rot["p"] = (i + 1) % rot["np"]
            return i

        def next_obank():
            i = rot["o"]; rot["o"] = (i + 1) % rot["no"]
            return 8 - rot["no"] + i

        def MM(out, lhsT, rhs, start, stop, reads, writes, inc):
            return P.op("pe", lambda e: e.matmul(out, lhsT=lhsT, rhs=rhs, start=start, stop=stop),
                        reads=reads, writes=writes, inc=inc)

        def TR(out, in_, reads, writes, inc):
            return P.op("pe", lambda e: e.transpose(out=out, in_=in_, identity=ident[:]),
                        reads=list(reads) + [r_ident], writes=writes, inc=inc)

        def ACT(out, in_, func, reads, writes, scale=1.0, bias=0.0, accum_out=None):
            if accum_out is None:
                return P.op("act", lambda e: e.activation(out=out, in_=in_, func=func, scale=scale, bias=bias),
                            reads=reads, writes=writes)
            return P.op("act", lambda e: e.activation(out=out, in_=in_, func=func, scale=scale, bias=bias,
                                                      accum_out=accum_out), reads=reads, writes=writes)

        def TS(eng, out, in0, s1, s2, op0, op1, reads, writes):
            if s2 is None:
                return P.op(eng, lambda e: e.tensor_scalar(out=out, in0=in0, scalar1=s1, scalar2=None, op0=op0),
                            reads=reads, writes=writes)
            return P.op(eng, lambda e: e.tensor_scalar(out=out, in0=in0, scalar1=s1, scalar2=s2, op0=op0, op1=op1),
                        reads=reads, writes=writes)

        def TT(eng, out, in0, in1, op, reads, writes):
            return P.op(eng, lambda e: e.tensor_tensor(out=out, in0=in0, in1=in1, op=op), reads=reads, writes=writes)

        def STT(eng, out, in0, scalar, in1, op0, op1, reads, writes):
            return P.op(eng, lambda e: e.scalar_tensor_tensor(out=out, in0=in0, scalar=scalar, in1=in1, op0=op0, op1=op1),
                        reads=reads, writes=writes)

        def CP(eng, out, in_, reads, writes):
            return P.op(eng, lambda e: e.tensor_copy(out=out, in_=in_), reads=reads, writes=writes)

        def MS(eng, ap, val, writes):
            return P.op(eng, lambda e: e.memset(ap, val), writes=writes)

        def wload(dst, src, res, eng="pool"):
            return P.dma(eng, dst, src, writes=[res])

        out_toks = []

        def dump(name, ap, shape, dt, res_list):
            if not debug:
                return
            d = nc.dram_tensor("dbg_" + name, shape, dt, kind="ExternalOutput").ap()
            dbg[name] = d
            r = P.res("dbg_" + name)
            out_toks.append(P.dma("sp", d, ap, reads=list(res_list) + [r]))

        MS("pool", cf[:], 0.0, [r_cf])
        P.op("pool", lambda e: e.affine_select(out=cf[:], in_=cf[:], pattern=[[-1, 128]], compare_op=ALU.not_equal,
                                               fill=1.0, base=0, channel_multiplier=1), reads=[r_cf], writes=[r_cf])
        CP("pool", ident[:], cf[:], [r_cf], [r_ident])
        CP("pool", identf[:], cf[:], [r_cf], [r_identf])
        MS("pool", onesf[:], 1.0, [r_onesf])
        CP("pool", onesb[:], onesf[:], [r_onesf], [r_onesb])
        P.op("pool", lambda e: e.affine_select(out=cf[:], in_=onesf[:], pattern=[[1, 128]], compare_op=ALU.is_ge,
                                               fill=0.0, base=0, channel_multiplier=-1), reads=[r_onesf], writes=[r_cf])
        CP("pool", tri[:], cf[:], [r_cf], [r_tri])

        P.dma("sp", gmask[:], gmask_d.rearrange("p (a b) -> p a b", a=16), writes=[r_gmask])
        P.dma("sp", vmask[:], vmask_d.rearrange("p (a b) -> p a b", a=16), writes=[r_vmask])
        P.dma("sp", cT[:], c_row.rearrange("(k p) -> p k", p=128), writes=[r_cT], allow_slow_non_contiguous=True)
        for i, v in enumerate((norm_mix_g, norm_ffn_g, ln_v_g, ln_v_b)):
            P.dma("sp", gvec[:, i, :], v.rearrange("(k p) -> p k", p=128), writes=[r_gvec],
                  allow_slow_non_contiguous=True)
        ACT(cact[:], cT[:], AF.Silu, [r_cT], [r_cact])

        wada_v = w_ada.rearrange("(k p) n -> p k n", p=128)
        wst = [view(108 + 8 * i, [128, 8, 512], BF16) for i in range(4)]
        r_wst = [P.res(f"wst{i}") for i in range(4)]
        cactb = sbt("cactb", [128, 8], BF16); r_cactb = P.res("cactb")
        CP("dve", cactb[:], cact[:], [r_cact], [r_cactb])
        modrow = view(140, [128, 6144], F32)
        r_modrow = P.res("modrow")
        r_bada = P.res("bada")
        bada_row = view(164, [128, 6144], F32)
        P.dma("sp", bada_row[0:1, :], b_ada.rearrange("(o n) -> o n", o=1), writes=[r_bada])

        mod_piece_state = {"j": 0}

        def mod_piece():
            j = mod_piece_state["j"]
            if j >= 12:
                return False
            mod_piece_state["j"] = j + 1
            s = j % 4
            P.dma("pool", wst[s], wada_v[:, :, j * 512:(j + 1) * 512], writes=[r_wst[s]])
            b = next_bank()
            for k in range(8):
                MM(bank_ap[b][0:1, :], cactb[:, k:k + 1], wst[s][:, k, :], k == 0, k == 7,
                   [r_cactb, r_wst[s]], [r_bank[b]], k == 7)
            TT("dve", modrow[0:1, j * 512:(j + 1) * 512], bank_ap[b][0:1, :], bada_row[0:1, j * 512:(j + 1) * 512], ALU.add,
               [r_bank[b], r_bada], [r_modrow])
            return True

        def mod_transpose(j0, j1):
            b = next_bank()
            for j in range(j0, j1):
                MM(bank_ap[b][:, j:j + 1], modrow[0:1, j * 128:(j + 1) * 128], onesf[0:1, 0:1], True, True,
                   [r_modrow, r_onesf], [r_bank[b]], j == j1 - 1)
            CP("dve", modT[:, j0:j1], bank_ap[b][:, j0:j1], [r_bank[b]], [r_modT])

        for _ in range(4):
            mod_piece()
        mod_transpose(0, 16)
        STT("dve", AB[:, 0, :], modT[:, 8:16], 1.0, gvec[:, 0, :], ALU.add, ALU.mult, [r_modT, r_gvec], [r_AB])
        CP("dve", AB[:, 1, :], modT[:, 0:8], [r_modT], [r_AB])

        if debug:
            dump("AB", AB[:], [128, 4, 8], F32, [r_AB])

        hT_own = view(0, [128, 8, NT], BF16)
        hT_ctx = view(32, [128, 8, NT], BF16)
        r_hown = [P.res(f"hown{g}") for g in range(4)]
        r_hctx = [P.res(f"hctx{g}") for g in range(4)]
        xsl = [view(76 + 4 * i, [128, 1024], F32) for i in range(8)]
        r_xsl = [P.res(f"xsl{i}") for i in range(8)]
        xnb = [view(64 + 2 * i, [128, 1024], BF16) for i in range(2)]
        r_xnb = [P.res(f"xnb{i}") for i in range(2)]

        def norm_to_hT(src_dram, dstT, r_dst, scol, A, Bv, interleave=None):
            for g in range(2):
                MS("dve", ss[:, scol + g * 8: scol + g * 8 + 8], 0.0, [r_ss])
                for i in range(8):
                    t = g * 8 + i
                    P.dma("sp", xsl[i], src_dram[t * 128:(t + 1) * 128, :], writes=[r_xsl[i]])
                    if t % 2 == 0:
                        ACT(junk[:], xsl[i], AF.Square, [r_xsl[i]], [r_junk, r_ss],
                            accum_out=ss[:, scol + t: scol + t + 1])
                    else:
                        P.op("dve", lambda e, i=i, t=t: e.scalar_tensor_tensor(
                            out=junk2[:], in0=xsl[i], scalar=1.0, in1=xsl[i], op0=ALU.mult, op1=ALU.mult,
                            accum_out=ss[:, scol + t: scol + t + 1]), reads=[r_xsl[i]], writes=[r_junk2, r_ss])
                c0 = scol + g * 8
                TS("dve", sq[:, c0:c0 + 8], ss[:, c0:c0 + 8], 1.0 / D, EPS, ALU.mult, ALU.add, [r_ss], [r_sq])
                ACT(sq[:, c0:c0 + 8], sq[:, c0:c0 + 8], AF.Sqrt, [r_sq], [r_sq])
                P.op("dve", lambda e, c0=c0: e.reciprocal(out=sq[:, c0:c0 + 8], in_=sq[:, c0:c0 + 8]),
                     reads=[r_sq], writes=[r_sq])
                for i in range(8):
                    t = g * 8 + i
                    xb = t % 2
                    TS("dve", xnb[xb], xsl[i], sq[:, scol + t: scol + t + 1], None, ALU.mult, None,
                       [r_xsl[i], r_sq], [r_xnb[xb]])
                    b = next_bank()
                    pv = bank_bf[b].rearrange("p (k n) -> p k n", k=8)
                    for k in range(8):
                        TR(pv[:, k, :], xnb[xb][:, k * 128:(k + 1) * 128], [r_xnb[xb]], [r_bank[b]], k == 7)
                    for k in range(8):
                        if k < 8:
                            ACT(dstT[:, k, t * 128:(t + 1) * 128], pv[:, k, :], AF.Identity,
                                [r_bank[b], r_AB], [r_dst[t // 4]], scale=A[:, k:k + 1], bias=Bv[:, k:k + 1])
                        else:
                            TS("dve", dstT[:, k, t * 128:(t + 1) * 128], pv[:, k, :], A[:, k:k + 1], Bv[:, k:k + 1],
                               ALU.mult, ALU.add, [r_bank[b], r_AB], [r_dst[t // 4]])
                    if interleave is not None and t % 2 == 1:
                        interleave()

        if upto >= 1:
            norm_to_hT(x_own, hT_own, r_hown, 0, AB[:, 0, :], AB[:, 1, :], interleave=mod_piece)
            norm_to_hT(x_ctx, hT_ctx, r_hctx, 16, AB[:, 0, :], AB[:, 1, :], interleave=mod_piece)
            while mod_piece():
                pass
            mod_transpose(16, 48)
            STT("dve", AB[:, 2, :], modT[:, 32:40], 1.0, gvec[:, 1, :], ALU.add, ALU.mult, [r_modT, r_gvec], [r_AB])
            CP("dve", AB[:, 3, :], modT[:, 24:32], [r_modT], [r_AB])
            if debug and upto == 1:
                dump("modT", modT[:], [128, 48], F32, [r_modT])
                dump("hT_own", hT_own, [128, 8, NT], BF16, r_hown)
                dump("hT_ctx", hT_ctx, [128, 8, NT], BF16, r_hctx)


        def viewb(off_b, shape, dt):
            return view(off_b / 1024.0, shape, dt)

        KB = 1024
        ybT = view(76, [128, 8, NT], BF16)
        if upto >= 2:
            F2 = P.fence()
            r_ybT = [P.res(f"ybT{h}", after=F2) for h in range(8)]
            Vp = view(108, [128, 32, 4, 129], BF16); r_V = P.res("V", after=F2)
            qT = viewb(141 * KB, [128, NT], BF16); r_qT = P.res("qT", after=F2)
            kT = viewb(145 * KB, [128, 4096], BF16); r_kT = P.res("kT", after=F2)
            Oacc = viewb(153 * KB, [128, 16, 129], F32)
            r_Oacc = [P.res(f"Oacc{i}", after=F2) for i in range(16)]
            Pbuf = [viewb(161 * KB + 256 + 2048 * i, [128, 2, 512], BF16) for i in range(3)]
            r_Pbuf = [P.res(f"Pbuf{i}", after=F2) for i in range(3)]
            gm = viewb(167 * KB + 256, [128, 16, 16], F32); r_gm = P.res("gm", after=F2)
            sel = viewb(168 * KB + 256, [128, 16, 16], F32); r_sel = P.res("sel", after=F2)
            mx8 = viewb(169 * KB + 256, [128, 16, 8], F32); r_mx8 = P.res("mx8", after=F2)
            ksum = viewb(169 * KB + 768, [128, 16], F32); r_ksum = P.res("ksum", after=F2)
            kbar = viewb(169 * KB + 832, [128, 16], BF16); r_kbar = P.res("kbar", after=F2)
            rec = viewb(169 * KB + 896, [128, 16, 1], F32); r_rec = P.res("rec", after=F2)
            ynorm = viewb(170 * KB + 256, [128, 16, 128], BF16); r_ynorm = P.res("ynorm", after=F2)
            wq = [[viewb((176 + 8 * s) * KB + 2048 * hp, [128, 8, 128], BF16) for hp in range(2)] for s in range(2)]
            wk = [[viewb((176 + 8 * s) * KB + 4096 + 2048 * hp, [128, 8, 128], BF16) for hp in range(2)] for s in range(2)]
            wv4 = view(64, [128, 8, 512], BF16); r_wv4 = P.res("wv4", after=P.inherit(*r_xnb))
            r_wq = [[P.res(f"wq{s}{hp}", after=P.inherit(r_bada)) for hp in range(2)] for s in range(2)]
            r_wk = [[P.res(f"wk{s}{hp}", after=P.inherit(r_bada)) for hp in range(2)] for s in range(2)]
            win_v = w_in.rearrange("(k p) n -> p k n", p=128)

            def load_pair_weights(p):
                s = p % 2
                for hp in range(2):
                    h = 2 * p + hp
                    wload(wq[s][hp], win_v[:, :, 2048 + h * 128: 2048 + (h + 1) * 128], r_wq[s][hp])
                    wload(wk[s][hp], win_v[:, :, 3072 + h * 128: 3072 + (h + 1) * 128], r_wk[s][hp])

            def load_group_v(g):
                wload(wv4, win_v[:, :, 4096 + g * 512: 4096 + (g + 1) * 512], r_wv4)

            def hsrc(lt):
                if lt < 16:
                    return hT_ctx, r_hctx[lt // 4], lt
                return hT_own, r_hown[(lt - 16) // 4], lt - 16

            MS("pool", Vp[:, :, :, 128:129], 1.0, [r_V])
            load_pair_weights(0)
            load_group_v(0)
            set_rot(4, 2, 4)
            npairs = 4 if upto > 2 or not debug else int(dbg.get("_npairs", 4))
            for p in range(4):
                s = p % 2
                if p % 2 == 0:
                    for lt in range(32):
                        b = next_bank()
                        src, rs, ti = hsrc(lt)
                        for k in range(8):
                            MM(bank_ap[b], src[:, k, ti * 128:(ti + 1) * 128], wv4[:, k, :],
                               k == 0, k == 7, [rs, r_wv4], [r_bank[b]], k == 7)
                        o_ap = Vp[:, lt, :, 0:128]
                        i_ap = bank_ap[b].rearrange("p (h d) -> p h d", h=4)
                        if lt % 2 == 0:
                            CP("dve", o_ap, i_ap, [r_bank[b]], [r_V])
                        else:
                            ACT(o_ap, i_ap, AF.Identity, [r_bank[b]], [r_V])
                    if p == 0:
                        load_group_v(1)
                if debug and upto == 2 and p == 0:
                    dump("V0", Vp, [128, 32, 4, 129], BF16, [r_V])
                for hp in range(2):
                    h = 2 * p + hp
                    for tg in range(4):
                        b = next_bank()
                        for k in range(8):
                            MM(bank_ap[b], wq[s][hp][:, k, :], hT_own[:, k, tg * 512:(tg + 1) * 512], k == 0, k == 7,
                               [r_wq[s][hp], r_hown[tg]], [r_bank[b]], k == 7)
                        TS("dve", qT[:, tg * 512:(tg + 1) * 512], bank_ap[b], QSCALE, None, ALU.mult, None,
                           [r_bank[b]], [r_qT])
                    MS("dve", ksum[:], 0.0, [r_ksum])
                    for tg in range(8):
                        src, rs = (hT_ctx, r_hctx[tg]) if tg < 4 else (hT_own, r_hown[tg - 4])
                        tl = tg % 4
                        b = next_bank()
                        for k in range(8):
                            MM(bank_ap[b], wk[s][hp][:, k, :], src[:, k, tl * 512:(tl + 1) * 512], k == 0, k == 7,
                               [r_wk[s][hp], rs], [r_bank[b]], k == 7)
                        for j in range(2):
                            ACT(kT[:, tg * 512 + j * 256: tg * 512 + (j + 1) * 256], bank_ap[b][:, j * 256:(j + 1) * 256],
                                AF.Identity, [r_bank[b]], [r_kT, r_ksum], accum_out=ksum[:, 2 * tg + j: 2 * tg + j + 1])
                    TS("dve", kbar[:], ksum[:], 1.0 / 256, None, ALU.mult, None, [r_ksum], [r_kbar])
                    if h == 7 and upto >= 3:
                        wpb = view(32, [128, 8, 1024], BF16); r_wpb = P.res("wpb", after=P.inherit(*r_hctx))
                        wgb = view(48, [128, 8, 1024], BF16); r_wgb = P.res("wgb", after=P.inherit(*r_hctx))
                        wload(wpb, w_proj_b.rearrange("(k p) n -> p k n", p=128), r_wpb)
                        wload(wgb, win_v[:, :, 6144:7168], r_wgb)
                    if debug and upto == 2 and p == 0:
                        dump(f"qT{h}", qT, [128, NT], BF16, [r_qT])
                        dump(f"kT{h}", kT, [128, 4096], BF16, [r_kT])
                    b = next_bank()
                    for qt in range(16):
                        MM(bank_ap[b][:, qt * 16:(qt + 1) * 16], qT[:, qt * 128:(qt + 1) * 128], kbar[:], True, True,
                           [r_qT, r_kbar], [r_bank[b]], qt == 15)
                    TT("dve", gm[:], bank_ap[b][:, 0:256].rearrange("p (a b) -> p a b", a=16), gmask[:], ALU.add,
                       [r_bank[b], r_gmask], [r_gm])
                    for qt in range(16):
                        P.op("dve", lambda e, qt=qt: e.max(out=mx8[:, qt, :], in_=gm[:, qt, :]), reads=[r_gm], writes=[r_mx8])
                    for qt in range(16):
                        STT("dve", sel[:, qt, :], gm[:, qt, :], mx8[:, qt, 2:3], vmask[:, qt, :], ALU.is_ge, ALU.mult,
                            [r_gm, r_mx8, r_vmask], [r_sel])
                    if debug and upto == 2 and p == 0 and hp == 0:
                        dump("sel0", sel, [128, 16, 16], F32, [r_sel])
                        dump("gm0", gm, [128, 16, 16], F32, [r_gm])

                    units = []

                    def diag_unit(j):
                        bb = 8 + j
                        qa, qb = 2 * j, 2 * j + 1
                        kt0, kt1 = 2 * bb, 2 * bb + 1
                        stt = {}

                        def A():
                            pr = next_pair()
                            pbi = stt["pb"]
                            pb = Pbuf[pbi]
                            MM(pp[pr][:, 0, 0:256], kT[:, kt0 * 128:(kt0 + 1) * 128], qT[:, qa * 128:(qa + 2) * 128],
                               True, True, [r_kT, r_qT], [r_bank[2 * pr]], False)
                            MM(pp[pr][:, 1, 0:256], kT[:, kt1 * 128:(kt1 + 1) * 128], qT[:, qa * 128:(qa + 2) * 128],
                               True, True, [r_kT, r_qT], [r_bank[2 * pr + 1]], True)
                            ACT(pb[:, :, 0:256], pp[pr][:, :, 0:256], AF.Exp, [r_bank[2 * pr], r_bank[2 * pr + 1]],
                                [r_Pbuf[pbi]])
                            TT("dve", pb[:, 0, 0:128], pb[:, 0, 0:128], tri[:], ALU.mult, [r_Pbuf[pbi], r_tri], [r_Pbuf[pbi]])
                            TT("dve", pb[:, 1, 128:256], pb[:, 1, 128:256], tri[:], ALU.mult, [r_Pbuf[pbi], r_tri],
                               [r_Pbuf[pbi]])

                        def B():
                            pbi = stt["pb"]
                            pb = Pbuf[pbi]
                            ob = next_obank()
                            MM(bank_ap[ob][:, 0:129], pb[:, 0, 0:128], Vp[:, kt0, h % 4, :], True, True,
                               [r_Pbuf[pbi], r_V], [r_bank[ob]], True)
                            CP("dve", Oacc[:, qa, :], bank_ap[ob][:, 0:129], [r_bank[ob]], [r_Oacc[qa]])
                            ob = next_obank()
                            MM(bank_ap[ob][:, 0:129], pb[:, 0, 128:256], Vp[:, kt0, h % 4, :], True, False,
                               [r_Pbuf[pbi], r_V], [r_bank[ob]], False)
                            MM(bank_ap[ob][:, 0:129], pb[:, 1, 128:256], Vp[:, kt1, h % 4, :], False, True,
                               [r_Pbuf[pbi], r_V], [r_bank[ob]], True)
                            CP("dve", Oacc[:, qb, :], bank_ap[ob][:, 0:129], [r_bank[ob]], [r_Oacc[qb]])
                        return stt, A, B

                    def past_unit(bb, c0, ntile):
                        kt0, kt1 = 2 * bb, 2 * bb + 1
                        ncols = ntile * 128
                        stt = {}

                        def A():
                            pr = next_pair()
                            pbi = stt["pb"]
                            pb = Pbuf[pbi]
                            MM(pp[pr][:, 0, 0:ncols], kT[:, kt0 * 128:(kt0 + 1) * 128], qT[:, c0 * 128: c0 * 128 + ncols],
                               True, True, [r_kT, r_qT], [r_bank[2 * pr]], False)
                            MM(pp[pr][:, 1, 0:ncols], kT[:, kt1 * 128:(kt1 + 1) * 128], qT[:, c0 * 128: c0 * 128 + ncols],
                               True, True, [r_kT, r_qT], [r_bank[2 * pr + 1]], True)
                            ACT(pb[:, :, 0:ncols], pp[pr][:, :, 0:ncols], AF.Exp, [r_bank[2 * pr], r_bank[2 * pr + 1]],
                                [r_Pbuf[pbi]])

                        def B():
                            pbi = stt["pb"]
                            pb = Pbuf[pbi]
                            for i in range(ntile):
                                qt = c0 + i
                                ob = next_obank()
                                MM(bank_ap[ob][:, 0:129], pb[:, 0, i * 128:(i + 1) * 128], Vp[:, kt0, h % 4, :], True, False,
                                   [r_Pbuf[pbi], r_V], [r_bank[ob]], False)
                                MM(bank_ap[ob][:, 0:129], pb[:, 1, i * 128:(i + 1) * 128], Vp[:, kt1, h % 4, :], False, True,
                                   [r_Pbuf[pbi], r_V], [r_bank[ob]], True)
                                STT("dve", Oacc[:, qt, :], bank_ap[ob][:, 0:129], sel[:, qt, bb:bb + 1], Oacc[:, qt, :],
                                    ALU.mult, ALU.add, [r_bank[ob], r_sel, r_Oacc[qt]], [r_Oacc[qt]])
                        return stt, A, B

                    for j in range(8):
                        units.append(diag_unit(j))
                    for bb in range(15):
                        qs = 2 * bb if bb < 8 else 2 * (bb - 8 + 1)
                        for c0 in range(qs, 16, 4):
                            units.append(past_unit(bb, c0, min(4, 16 - c0)))
                    for ui, (stt, A, B) in enumerate(units):
                        stt["pb"] = ui % 3
                    DEPTH = 1
                    for ui in range(len(units) + DEPTH):
                        if ui < len(units):
                            units[ui][1]()
                        if ui >= DEPTH:
                            units[ui - DEPTH][2]()

                    P.op("dve", lambda e: e.reciprocal(out=rec[:], in_=Oacc[:, :, 128:129]), reads=r_Oacc, writes=[r_rec])
                    TT("dve", ynorm[:], Oacc[:, :, 0:128], rec[:].to_broadcast([128, 16, 128]), ALU.mult,
                       list(r_Oacc) + [r_rec], [r_ynorm])
                    for hf in range(2):
                        b = next_bank()
                        pv = bank_bf[b].rearrange("p (k n) -> p k n", k=8)
                        for i in range(8):
                            TR(pv[:, i, :], ynorm[:, hf * 8 + i, :], [r_ynorm], [r_bank[b]], i == 7)
                        if hf == 0:
                            CP("dve", ybT[:, h, hf * 1024:(hf + 1) * 1024], bank_bf[b], [r_bank[b]], [r_ybT[h]])
                        else:
                            ACT(ybT[:, h, hf * 1024:(hf + 1) * 1024], bank_bf[b], AF.Identity, [r_bank[b]], [r_ybT[h]])
                    if hp == 0 and p + 1 < 4:
                        load_pair_weights(p + 1)
            set_rot(8, 4, 2)
            if debug and upto == 2:
                dump("ybT", ybT, [128, 8, NT], BF16, r_ybT)


        def bcast_row_from_modT(dst, col0, res_dst):
            pr = next_pair()
            for c in range(8):
                TS("dve", cf[:], identf[:], modT[:, col0 + c: col0 + c + 1], None, ALU.mult, None,
                   [r_identf, r_modT], [r_cf])
                MM(pp[pr][:, c // 4, (c % 4) * 128:(c % 4 + 1) * 128], onesf[:], cf[:], True, True,
                   [r_onesf, r_cf], [r_bank[2 * pr + c // 4]], True)
            CP("dve", dst, pp[pr][:, :, :].rearrange("p a b -> p (a b)"), [r_bank[2 * pr], r_bank[2 * pr + 1]], [res_dst])

        merged = view(108, [128, 8, NT], BF16)
        if upto >= 3:
            F3 = P.fence()
            r_merged = [P.res(f"merged{g}", after=F3) for g in range(4)]
            sgt = [view(72 + 2 * i, [128, 512], F32) for i in range(2)]
            r_sgt = [P.res(f"sgt{i}", after=F3) for i in range(2)]
            wu_a = view(140, [128, 8, 1024], BF16); r_wu_a = P.res("wu_a", after=F3)
            wvg = view(156, [128, 8, 1024], BF16); r_wvg = P.res("wvg", after=F3)
            wpa = view(172, [128, 8, 1024], BF16); r_wpa = P.res("wpa", after=F3)
            wload(wu_a, win_v[:, :, 0:1024], r_wu_a)
            wload(wvg, win_v[:, :, 1024:2048], r_wvg)
            wload(wpa, w_proj_a.rearrange("(k p) n -> p k n", p=128), r_wpa)
            wsf = view(188, [128, 8, 128], F32); r_wsf = P.res("wsf", after=F3)
            wsb = view(64, [128, 8, 128], BF16); r_wsb = P.res("wsb", after=F3)
            bsb = view(68, [128, 8, 128], F32); r_bsb = P.res("bsb", after=F3)
            P.dma("sp", wsf, w_spatial.rearrange("g t s -> t g s"), writes=[r_wsf])
            P.dma("sp", bsb.rearrange("p a b -> p (a b)"), b_spatial.rearrange("g t -> (g t)").partition_broadcast(128),
                  writes=[r_bsb])
            P.op("pool", lambda e: e.affine_select(out=wsf, in_=wsf, pattern=[[0, 8], [-1, 128]], compare_op=ALU.is_ge,
                                                   fill=0.0, base=0, channel_multiplier=1), reads=[r_wsf], writes=[r_wsf])
            CP("pool", wsb, wsf, [r_wsf], [r_wsb])
            b = next_bank()
            pv = bank_bf[b].rearrange("p (k n) -> p k n", k=8)
            for g in range(8):
                TR(pv[:, g, :], wsb[:, g, :], [r_wsb], [r_bank[b]], g == 7)
            CP("dve", wsT[:], pv, [r_bank[b]], [r_wsT])
            pr = next_pair()
            for g in range(8):
                MM(pp[pr][:, g // 4, (g % 4) * 128:(g % 4 + 1) * 128], onesb[:], wsT[:, g, :], True, True,
                   [r_onesb, r_wsT], [r_bank[2 * pr + g // 4]], True)
            for g in range(8):
                STT("dve", Ct[:, g, :], pp[pr][:, g // 4, (g % 4) * 128:(g % 4 + 1) * 128], gvec[:, 3, g:g + 1], bsb[:, g, :],
                    ALU.mult, ALU.add, [r_bank[2 * pr + g // 4], r_gvec, r_bsb], [r_Ct])

            n = 0
            for tg in range(4):
                for c in range(8):
                    bp = next_bank(); bg = next_bank()
                    for k in range(8):
                        MM(bank_ap[bp], wpb[:, k, c * 128:(c + 1) * 128], ybT[:, k, tg * 512:(tg + 1) * 512], k == 0, k == 7,
                           [r_wpb, r_ybT[k]], [r_bank[bp]], k == 7)
                    for k in range(8):
                        MM(bank_ap[bg], wgb[:, k, c * 128:(c + 1) * 128], hT_own[:, k, tg * 512:(tg + 1) * 512], k == 0, k == 7,
                           [r_wgb, r_hown[tg]], [r_bank[bg]], k == 7)
                    si = n % 2; n += 1
                    ACT(sgt[si], bank_ap[bg], AF.Sigmoid, [r_bank[bg]], [r_sgt[si]])
                    TT("dve", merged[:, c, tg * 512:(tg + 1) * 512], sgt[si], bank_ap[bp], ALU.mult,
                       [r_sgt[si], r_bank[bp]], [r_merged[tg]])
            if debug and upto == 3:
                dump("merged", merged, [128, 8, NT], BF16, r_merged)

        if upto >= 4:
            F4 = P.fence()
            wga = view(76, [128, 8, 1024], BF16); r_wga = P.res("wga", after=F4)
            wload(wga, win_v[:, :, 5120:6144], r_wga)
            uT = view(92, [128, 8, 512], BF16); r_uT = P.res("uT", after=F4)
            yaT = view(100, [128, 8, 512], BF16); r_yaT = P.res("yaT", after=F4)
            gv = [view(44 + 4 * i, [128, 1024], F32) for i in range(4)]
            r_gv = [P.res(f"gv{i}", after=F4) for i in range(4)]
            tmpM = [view(32 + 4 * i, [128, 8, 128], F32) for i in range(2)]
            r_tmpM = [P.res(f"tmpM{i}", after=F4) for i in range(2)]
            sg2 = [view(40 + 2 * i, [128, 512], F32) for i in range(2)]
            r_sg2 = [P.res(f"sg2{i}", after=F4) for i in range(2)]

            F4b = P.fence()
            uTs = [uT, view(68, [128, 8, 512], BF16)]
            r_uTs = [r_uT, P.res("uT1", after=F4b)]
            vhat = [view(60 + 2 * i, [128, 1024], BF16) for i in range(4)]
            r_vhat = [P.res(f"vhat{i}", after=F4b) for i in range(4)]
            junkf = junk2[:].bitcast(F32)
            r_junkf = r_junk2
            stb = [st4, cf]
            r_st8 = [r_st4, r_cf]
            cnt4 = {"n": 0}

            def U(tg, cs=range(8)):
                ut, rut = uTs[tg % 2], r_uTs[tg % 2]
                for c in cs:
                    b = next_bank()
                    for k in range(8):
                        MM(bank_ap[b], wu_a[:, k, c * 128:(c + 1) * 128], hT_own[:, k, tg * 512:(tg + 1) * 512], k == 0, k == 7,
                           [r_wu_a, r_hown[tg]], [r_bank[b]], k == 7)
                    ACT(ut[:, c, :], bank_ap[b], AF.Gelu_apprx_tanh, [r_bank[b]], [rut])

            def V(tg, tiles=range(4), init=True):
                o = 0; rs = r_st8[tg % 2]; st8 = stb[tg % 2]
                if init:
                    MS("dve", st8[:, o:o + 16], 0.0, [rs])
                for i in tiles:
                    t = 4 * tg + i
                    pr = next_pair()
                    for hf in range(2):
                        for k in range(8):
                            MM(pp[pr][:, hf, :], hT_own[:, k, t * 128:(t + 1) * 128], wvg[:, k, hf * 512:(hf + 1) * 512],
                               k == 0, k == 7, [r_hown[tg], r_wvg], [r_bank[2 * pr + hf]], k == 7)
                        ACT(gv[i][:, hf * 512:(hf + 1) * 512], pp[pr][:, hf, :], AF.Gelu_apprx_tanh, [r_bank[2 * pr + hf]],
                            [r_gv[i], rs], accum_out=st8[:, o + 2 * i + hf: o + 2 * i + hf + 1])
                    for hf in range(2):
                        P.op("dve", lambda e, i=i, hf=hf, o=o, st8=st8: e.scalar_tensor_tensor(
                            out=junkf, in0=gv[i][:, hf * 512:(hf + 1) * 512], scalar=1.0, in1=gv[i][:, hf * 512:(hf + 1) * 512],
                            op0=ALU.mult, op1=ALU.mult, accum_out=st8[:, o + 8 + 2 * i + hf: o + 8 + 2 * i + hf + 1]),
                            reads=[r_gv[i]], writes=[r_junkf, rs])

            def STATS_VH(tg):
                o = 0; rs = r_st8[tg % 2]; st8 = stb[tg % 2]
                s1 = st8[:, o:o + 8].rearrange("p (i h) -> p i h", h=2)
                s2 = st8[:, o + 8:o + 16].rearrange("p (i h) -> p i h", h=2)
                c = lambda a: st8[:, o + a:o + a + 4]
                TT("dve", c(16), s1[:, :, 0], s1[:, :, 1], ALU.add, [rs], [rs])
                TT("dve", c(20), s2[:, :, 0], s2[:, :, 1], ALU.add, [rs], [rs])
                TS("dve", c(24), c(16), 1.0 / D, None, ALU.mult, None, [rs], [rs])
                TT("dve", c(28), c(24), c(24), ALU.mult, [rs], [rs])
                STT("dve", c(32), c(20), 1.0 / D, c(28), ALU.mult, ALU.subtract, [rs], [rs])
                TS("dve", c(32), c(32), EPS, None, ALU.add, None, [rs], [rs])
                ACT(c(36), c(32), AF.Sqrt, [rs], [rs])
                P.op("dve", lambda e: e.reciprocal(out=c(40), in_=c(36)), reads=[rs], writes=[rs])
                STT("dve", c(44), c(24), -1.0, c(40), ALU.mult, ALU.mult, [rs], [rs])
                for i in range(4):
                    TS("dve", vhat[i], gv[i], st8[:, o + 40 + i:o + 41 + i], st8[:, o + 44 + i:o + 45 + i], ALU.mult, ALU.add,
                       [r_gv[i], rs], [r_vhat[i]])

            def SP(tg, tiles=range(4)):
                ut, rut = uTs[tg % 2], r_uTs[tg % 2]
                for i in tiles:
                    vb = i % 2
                    pr = next_pair()
                    for g in range(8):
                        MM(pp[pr][:, g // 4, (g % 4) * 128:(g % 4 + 1) * 128], vhat[i][:, g * 128:(g + 1) * 128], wsT[:, g, :],
                           True, True, [r_vhat[i], r_wsT], [r_bank[2 * pr + g // 4]], g == 3 or g == 7)
                    mt = tmpM[vb]
                    TT("dve", mt, pp[pr][:, :, :].rearrange("p a (b c) -> p (a b) c", b=4),
                       gvec[:, 2, :].unsqueeze(2).to_broadcast([128, 8, 128]), ALU.mult,
                       [r_bank[2 * pr], r_bank[2 * pr + 1], r_gvec], [r_tmpM[vb]])
                    TT("dve", mt, mt, Ct[:], ALU.add, [r_tmpM[vb], r_Ct], [r_tmpM[vb]])
                    TT("dve", yaT[:, :, i * 128:(i + 1) * 128], mt, ut[:, :, i * 128:(i + 1) * 128], ALU.mult,
                       [r_tmpM[vb], rut], [r_yaT])

            def PAGA(tg):
                for c in range(8):
                    bp = next_bank(); bg = next_bank()
                    for k in range(8):
                        MM(bank_ap[bp], wpa[:, k, c * 128:(c + 1) * 128], yaT[:, k, :], k == 0, k == 7,
                           [r_wpa, r_yaT], [r_bank[bp]], k == 7)
                    for k in range(8):
                        MM(bank_ap[bg], wga[:, k, c * 128:(c + 1) * 128], hT_own[:, k, tg * 512:(tg + 1) * 512], k == 0, k == 7,
                           [r_wga, r_hown[tg]], [r_bank[bg]], k == 7)
                    si = cnt4["n"] % 2; cnt4["n"] += 1
                    ACT(sg2[si], bank_ap[bg], AF.Sigmoid, [r_bank[bg]], [r_sg2[si]])
                    TT("dve", sg2[si], sg2[si], bank_ap[bp], ALU.mult, [r_sg2[si], r_bank[bp]], [r_sg2[si]])
                    TT("dve", merged[:, c, tg * 512:(tg + 1) * 512], merged[:, c, tg * 512:(tg + 1) * 512], sg2[si], ALU.add,
                       [r_merged[tg], r_sg2[si]], [r_merged[tg]])

            U(0); V(0); STATS_VH(0)
            for tg in range(4):
                for i in range(4):
                    SP(tg, [i])
                    if tg + 1 < 4:
                        U(tg + 1, [2 * i, 2 * i + 1])
                        V(tg + 1, [i], init=(i == 0))
                if tg == 2 and upto >= 5:
                    wo = view(140, [128, 8, 1024], BF16); r_wo = P.res("wo", after=P.inherit(r_wu_a))
                    wload(wo, w_out.rearrange("(k p) n -> p k n", p=128), r_wo)
                PAGA(tg)
                if tg + 1 < 4:
                    STATS_VH(tg + 1)
            if debug and upto == 4:
                dump("merged", merged, [128, 8, NT], BF16, r_merged)

        x1 = view(44, [128, 16, 1024], F32)
        if upto >= 5:
            F5 = P.fence()
            r_x1 = [P.res(f"x1_{t}", after=F5) for t in range(16)]
            xs = [view(156 + 4 * i, [128, 1024], F32) for i in range(3)]
            r_xs = [P.res(f"xs{i}", after=F5) for i in range(3)]
            gmb = view(168, [128, 1024], F32); r_gmb = P.res("gmb", after=F5)
            bcast_row_from_modT(gmb, 16, r_gmb)
            MS("dve", ss[:, 32:48], 0.0, [r_ss])
            for t in range(16):
                pr = next_pair()
                for hf in range(2):
                    for k in range(8):
                        MM(pp[pr][:, hf, :], merged[:, k, t * 128:(t + 1) * 128], wo[:, k, hf * 512:(hf + 1) * 512],
                           k == 0, k == 7, [r_merged[t // 4], r_wo], [r_bank[2 * pr + hf]], k == 7)
                xi = t % 3
                P.dma("sp", xs[xi], x_own[t * 128:(t + 1) * 128, :], writes=[r_xs[xi]])
                TT("dve", x1[:, t, :], pp[pr][:, :, :].rearrange("p a b -> p (a b)"), gmb, ALU.mult,
                   [r_bank[2 * pr], r_bank[2 * pr + 1], r_gmb], [r_x1[t]])
                TT("dve", x1[:, t, :], x1[:, t, :], xs[xi], ALU.add, [r_x1[t], r_xs[xi]], [r_x1[t]])
                if t % 2 == 0:
                    ACT(junk[:], x1[:, t, :], AF.Square, [r_x1[t]], [r_junk, r_ss], accum_out=ss[:, 32 + t:33 + t])
                else:
                    P.op("dve", lambda e, t=t: e.scalar_tensor_tensor(
                        out=junk2[:], in0=x1[:, t, :], scalar=1.0, in1=x1[:, t, :], op0=ALU.mult, op1=ALU.mult,
                        accum_out=ss[:, 32 + t:33 + t]), reads=[r_x1[t]], writes=[r_junk2, r_ss])
            if debug and upto == 5:
                dump("x1", x1, [128, 16, 1024], F32, r_x1)

        h2T = hT_own
        if upto >= 6:
            xn2 = [view(172 + 2 * i, [128, 1024], BF16) for i in range(2)]
            r_xn2 = [P.res(f"xn2{i}", after=F5) for i in range(2)]
            TS("dve", sq[:, 32:48], ss[:, 32:48], 1.0 / D, EPS, ALU.mult, ALU.add, [r_ss], [r_sq])
            ACT(sq[:, 32:48], sq[:, 32:48], AF.Sqrt, [r_sq], [r_sq])
            P.op("dve", lambda e: e.reciprocal(out=sq[:, 32:48], in_=sq[:, 32:48]), reads=[r_sq], writes=[r_sq])
            for t in range(16):
                xb = t % 2
                TS("dve", xn2[xb], x1[:, t, :], sq[:, 32 + t:33 + t], None, ALU.mult, None, [r_x1[t], r_sq], [r_xn2[xb]])
                b = next_bank()
                pv = bank_bf[b].rearrange("p (k n) -> p k n", k=8)
                for k in range(8):
                    TR(pv[:, k, :], xn2[xb][:, k * 128:(k + 1) * 128], [r_xn2[xb]], [r_bank[b]], k == 7)
                for k in range(8):
                    if k < 8:
                        ACT(h2T[:, k, t * 128:(t + 1) * 128], pv[:, k, :], AF.Identity, [r_bank[b], r_AB], [r_hown[t // 4]],
                            scale=AB[:, 2, k:k + 1], bias=AB[:, 3, k:k + 1])
                    else:
                        TS("dve", h2T[:, k, t * 128:(t + 1) * 128], pv[:, k, :], AB[:, 2, k:k + 1], AB[:, 3, k:k + 1],
                           ALU.mult, ALU.add, [r_bank[b], r_AB], [r_hown[t // 4]])
            if debug and upto == 6:
                dump("h2T", h2T, [128, 8, NT], BF16, r_hown)

        if upto >= 7:
            F7 = P.fence()
            aT = view(108, [128, 11, NT], BF16)
            r_aT = [P.res(f"aT{g}", after=F7) for g in range(4)]
            wd = view(152, [128, 11, 1024], BF16); r_wd = P.res("wd", after=F7)
            wgs = [view(174 + 4 * i, [128, 8, 128], BF16) for i in range(4)]
            wus = [view(176 + 4 * i, [128, 8, 128], BF16) for i in range(4)]
            r_wgs = [P.res(f"wgs{i}", after=F7) for i in range(4)]
            r_wus = [P.res(f"wus{i}", after=F7) for i in range(4)]
            gfb = view(32, [128, 1024], F32); r_gfb = P.res("gfb", after=F7)
            slt = [view(36 + 2 * i, [128, 512], F32) for i in range(2)]
            r_slt = [P.res(f"slt{i}", after=F7) for i in range(2)]
            tmpd = view(40, [128, 1024], F32); r_tmpd = P.res("tmpd", after=F7)
            wg_v = w_ffn_gate.rearrange("(k p) n -> p k n", p=128)
            wu_v = w_ffn_up.rearrange("(k p) n -> p k n", p=128)
            wd_v = w_ffn_down.rearrange("(c p) n -> p c n", p=128)

            def load_gu(cc):
                sl = cc % 4
                wload(wgs[sl], wg_v[:, :, cc * 128:(cc + 1) * 128], r_wgs[sl])
                wload(wus[sl], wu_v[:, :, cc * 128:(cc + 1) * 128], r_wus[sl])

            bcast_row_from_modT(gfb, 40, r_gfb)
            load_gu(0); load_gu(1)
            n = 0
            for hh in range(2):
                wload(wd, wd_v[:, hh * 11:(hh + 1) * 11, :], r_wd)
                for c in range(11):
                    cc = hh * 11 + c
                    if cc + 2 < NFC:
                        load_gu(cc + 2)
                    sl = cc % 4
                    for tg in range(4):
                        bg = next_bank(); bu = next_bank()
                        for k in range(8):
                            MM(bank_ap[bg], wgs[sl][:, k, :], h2T[:, k, tg * 512:(tg + 1) * 512], k == 0, k == 7,
                               [r_wgs[sl], r_hown[tg]], [r_bank[bg]], k == 7)
                        for k in range(8):
                            MM(bank_ap[bu], wus[sl][:, k, :], h2T[:, k, tg * 512:(tg + 1) * 512], k == 0, k == 7,
                               [r_wus[sl], r_hown[tg]], [r_bank[bu]], k == 7)
                        si = n % 2; n += 1
                        ACT(slt[si], bank_ap[bg], AF.Silu, [r_bank[bg]], [r_slt[si]])
                        TT("dve", aT[:, c, tg * 512:(tg + 1) * 512], slt[si], bank_ap[bu], ALU.mult,
                           [r_slt[si], r_bank[bu]], [r_aT[tg]])
                if hh == 1 and upto >= 8:
                    F8 = P.fence()
                    gfin = view(174, [128, 1024], F32); r_gfin = P.res("gfin", after=F8)
                    ost = [view(178 + 4 * i, [128, 1024], F32) for i in range(3)]
                    r_ost = [P.res(f"ost{i}", after=F8) for i in range(3)]
                    P.dma("sp", gfin, norm_final_g.partition_broadcast(128), writes=[r_gfin])
                    MS("dve", ss[:, 48:64], 0.0, [r_ss])

                    r_fs = [P.res(f"fs{t}") for t in range(16)]

                    def final_a(t):
                        ACT(junk[:], x1[:, t, :], AF.Square, [r_x1[t]], [r_junk, r_fs[t]], accum_out=ss[:, 48 + t:49 + t])

                    def final_b(t):
                        TS("dve", sq[:, 48 + t:49 + t], ss[:, 48 + t:49 + t], 1.0 / D, EPS, ALU.mult, ALU.add, [r_fs[t]], [r_fs[t]])
                        ACT(sq[:, 48 + t:49 + t], sq[:, 48 + t:49 + t], AF.Sqrt, [r_fs[t]], [r_fs[t]])
                        P.op("dve", lambda e, t=t: e.reciprocal(out=sq[:, 48 + t:49 + t], in_=sq[:, 48 + t:49 + t]),
                             reads=[r_fs[t]], writes=[r_fs[t]])
                        oi = t % 3
                        STT("dve", ost[oi], x1[:, t, :], sq[:, 48 + t:49 + t], gfin, ALU.mult, ALU.mult,
                            [r_x1[t], r_fs[t], r_gfin], [r_ost[oi]])
                        out_toks.append(P.dma("sp", out_d[t * 128:(t + 1) * 128, :], ost[oi], reads=[r_ost[oi]]))
                for t in range(16):
                    pr = next_pair()
                    for hf in range(2):
                        for c in range(11):
                            MM(pp[pr][:, hf, :], aT[:, c, t * 128:(t + 1) * 128], wd[:, c, hf * 512:(hf + 1) * 512],
                               c == 0, c == 10, [r_aT[t // 4], r_wd], [r_bank[2 * pr + hf]], c == 10)
                    TT("dve", tmpd, pp[pr][:, :, :].rearrange("p a b -> p (a b)"), gfb, ALU.mult,
                       [r_bank[2 * pr], r_bank[2 * pr + 1], r_gfb], [r_tmpd])
                    TT("dve", x1[:, t, :], x1[:, t, :], tmpd, ALU.add, [r_x1[t], r_tmpd], [r_x1[t]])
                    if hh == 1 and upto >= 8:
                        final_a(t)
                        if t >= 1:
                            final_b(t - 1)
                        if t == 15:
                            final_b(15)

        for t in out_toks:
            P.wait("sp", t)
        P.emit_all()
    return nc, dbg


def make_in_maps(inputs):
    x = np.ascontiguousarray(inputs["x"], dtype=np.float32)
    c = np.ascontiguousarray(inputs["c"], dtype=np.float32)
    shared = {
        "w_ada": inputs["w_ada"][0], "b_ada": inputs["b_ada"][0], "norm_mix_g": inputs["norm_mix_g"][0],
        "w_in": inputs["w_in"][0], "ln_v_g": inputs["ln_v_g"][0], "ln_v_b": inputs["ln_v_b"][0],
        "w_spatial": inputs["w_spatial"][0], "b_spatial": inputs["b_spatial"][0],
        "w_proj_a": inputs["w_proj_a"][0], "w_proj_b": inputs["w_proj_b"][0], "w_out": inputs["w_out"][0],
        "norm_ffn_g": inputs["norm_ffn_g"][0], "w_ffn_gate": inputs["w_ffn_gate"][0],
        "w_ffn_up": inputs["w_ffn_up"][0], "w_ffn_down": inputs["w_ffn_down"][0],
        "norm_final_g": inputs["norm_final_g"],
    }
    shared = {k: np.ascontiguousarray(np.asarray(v), dtype=np.float32) for k, v in shared.items()}
    in_maps = []
    for core in range(8):
        b, half = core // 2, core % 2
        gm = np.full((16, 16), NEG, dtype=np.float32)
        for qt in range(16):
            o = qt // 2
            for blk in range(16):
                if blk < 8:
                    ok = (blk < o) or (blk == o and half == 1)
                else:
                    ok = (blk - 8) < o
                if ok:
                    gm[qt, blk] = 0.0
        vm = (gm == 0.0).astype(np.float32)
        m = dict(shared)
        for name in PADDED:
            w = shared[name]
            wp = np.empty((w.shape[0] + 1, w.shape[1]), dtype=np.float32)
            wp[:-1] = w
            wp[-1] = float(core)
            m[name] = wp
        xb = x[b].reshape(16, 256, D)
        m["x_own"] = np.ascontiguousarray(xb[half::2].reshape(NT, D))
        m["x_ctx"] = np.ascontiguousarray(xb[1 - half::2].reshape(NT, D))
        m["c_row"] = np.ascontiguousarray(c[b])
        m["gmask"] = np.ascontiguousarray(np.broadcast_to(gm.reshape(1, 256), (128, 256)))
        m["vmask"] = np.ascontiguousarray(np.broadcast_to(vm.reshape(1, 256), (128, 256)))
        in_maps.append(m)
    return in_maps


PADDED = ("w_ada", "w_in", "w_proj_a", "w_proj_b", "w_out", "w_ffn_gate", "w_ffn_up", "w_ffn_down")
_NC_CACHE = {}


def kernel(**inputs):
    if "nc" not in _NC_CACHE:
        _NC_CACHE["nc"] = build_program()[0]
    nc = _NC_CACHE["nc"]
    in_maps = make_in_maps(inputs)
    res = run_bass_kernel_spmd(nc, in_maps, core_ids=list(range(8)))
    out = np.empty((4, 4096, D), dtype=np.float32)
    for core in range(8):
        b, half = core // 2, core % 2
        out[b].reshape(16, 256, D)[half::2] = np.asarray(res.results[core]["out"], dtype=np.float32).reshape(8, 256, D)
    return out
```

```python
import contextlib
import numpy as np
import concourse.bass as bass
import concourse.mybir as mybir
from concourse.bass_utils import run_bass_kernel_spmd

F32 = mybir.dt.float32
BF16 = mybir.dt.bfloat16
AF = mybir.ActivationFunctionType
ALU = mybir.AluOpType

D = 1024
NT = 2048
NTILE = 16
HD = 128
NH = 8
FH = 2816
NFC = 22
EPS = 1e-6
NEG = -1e30
QSCALE = float(HD ** -0.5)


class Res:
    __slots__ = ("name", "w", "r", "dsem", "dcnt")

    def __init__(self, name):
        self.name = name
        self.w = None
        self.r = {}
        self.dsem = None
        self.dcnt = 0


class Prog:
    ENGS = ("pe", "act", "dve", "pool", "sp")

    def __init__(self, nc, stack):
        self.nc = nc
        self.stack = stack
        self.ops = {e: [] for e in self.ENGS}
        self.sem = {e: stack.enter_context(nc.semaphore("s_" + e)) for e in self.ENGS}
        self.cnt = {e: 0 for e in self.ENGS}
        self.seen = {e: {} for e in self.ENGS}
        self.nres = 0
        self.dma_toks = {}
        self.free_dsems = []

    def res(self, name=None, after=None):
        self.nres += 1
        r = Res(f"{name or 'r'}_{self.nres}")
        if after:
            r.r = dict(after)
        return r

    def inherit(self, *olds):
        f = {}
        for i, o in enumerate(olds):
            if o.w is not None:
                f[f"I{i}w"] = o.w
            for k, t in o.r.items():
                f[f"I{i}{k}"] = t
        return f

    def fence(self):
        f = {}
        for e in ("pe", "act", "dve", "pool"):
            if self.cnt[e] > 0:
                f["F" + e] = (e, self.sem[e], self.cnt[e])
        for k, t in self.dma_toks.items():
            f["F" + k] = t
        return f

    def _need(self, eng, tok, waits, raw):
        if tok is None:
            return
        key, sem, val = tok
        if key == eng and (eng == "pe" or not raw or val <= self.cnt[eng] - 3):
            return
        if self.seen[eng].get(key, 0) >= val:
            return
        self.seen[eng][key] = val
        waits.append((sem, val))

    def _deps(self, eng, reads, writes):
        waits = []
        for r in reads:
            self._need(eng, r.w, waits, True)
        for r in writes:
            self._need(eng, r.w, waits, False)
            for t in r.r.values():
                self._need(eng, t, waits, False)
        return waits

    def op(self, eng, fn, reads=(), writes=(), inc=True):
        waits = self._deps(eng, reads, writes)
        n = self.cnt[eng] + 1
        if inc:
            self.cnt[eng] = n
        tok = (eng, self.sem[eng], n)
        for r in reads:
            r.r[eng] = tok
        for r in writes:
            r.w = tok
            r.r = {}
        sem = self.sem[eng]

        def emit(e, fn=fn, waits=waits, inc=inc, sem=sem):
            for (s, v) in waits[:-1]:
                e.wait_ge(s, v)
            ins = fn(e)
            if waits:
                ins._wait_ge(waits[-1][0], waits[-1][1])
            if inc:
                ins.then_inc(sem, 1)
        self.ops[eng].append(emit)
        return tok

    def dma(self, eng, out, in_, reads=(), writes=(), **kw):
        waits = self._deps(eng, reads, writes)
        owner = (list(writes) + list(reads))[0]
        if owner.dsem is None:
            owner.dsem = self.stack.enter_context(self.nc.semaphore("d_" + owner.name))
        owner.dcnt += 16
        key = "d_" + owner.name
        tok = (key, owner.dsem, owner.dcnt)
        self.dma_toks[key] = tok
        for r in reads:
            r.r[key] = tok
        for r in writes:
            r.w = tok
            r.r = {}
        sem = owner.dsem

        def emit(e, waits=waits, sem=sem, out=out, in_=in_, kw=kw):
            for (s, v) in waits[:-1]:
                e.wait_ge(s, v)
            ins = e.dma_start(out=out, in_=in_, **kw)
            if waits:
                ins._wait_ge(waits[-1][0], waits[-1][1])
            ins.then_inc(sem, 16)
        self.ops[eng].append(emit)
        return tok

    def wait(self, eng, tok):
        waits = []
        self._need(eng, tok, waits, True)
        if waits:
            def emit(e, waits=waits):
                for (s, v) in waits:
                    e.wait_ge(s, v)
            self.ops[eng].append(emit)

    def emit_all(self):
        nc = self.nc
        with nc.Block() as block:
            @block.tensor
            def _(e):
                for f in self.ops["pe"]:
                    f(e)

            @block.scalar
            def _(e):
                for f in self.ops["act"]:
                    f(e)

            @block.vector
            def _(e):
                for f in self.ops["dve"]:
                    f(e)

            @block.gpsimd
            def _(e):
                for f in self.ops["pool"]:
                    f(e)

            @block.sync
            def _(e):
                for f in self.ops["sp"]:
                    f(e)


def build_program(upto=99, debug=False):
    nc = bass.Bass("TRN2", target_bir_lowering=False)
    dr = lambda name, shape, dt=F32, kind="ExternalInput": nc.dram_tensor(name, shape, dt, kind=kind).ap()
    x_own = dr("x_own", [NT, D])
    x_ctx = dr("x_ctx", [NT, D])
    c_row = dr("c_row", [D])
    w_ada = dr("w_ada", [D + 1, 6 * D])[0:D, :]
    b_ada = dr("b_ada", [6 * D])
    norm_mix_g = dr("norm_mix_g", [D])
    w_in = dr("w_in", [D + 1, 7 * D])[0:D, :]
    ln_v_g = dr("ln_v_g", [D])
    ln_v_b = dr("ln_v_b", [D])
    w_spatial = dr("w_spatial", [8, 128, 128])
    b_spatial = dr("b_spatial", [8, 128])
    w_proj_a = dr("w_proj_a", [D + 1, D])[0:D, :]
    w_proj_b = dr("w_proj_b", [D + 1, D])[0:D, :]
    w_out = dr("w_out", [D + 1, D])[0:D, :]
    norm_ffn_g = dr("norm_ffn_g", [D])
    w_ffn_gate = dr("w_ffn_gate", [D + 1, FH])[0:D, :]
    w_ffn_up = dr("w_ffn_up", [D + 1, FH])[0:D, :]
    w_ffn_down = dr("w_ffn_down", [FH + 1, D])[0:FH, :]
    norm_final_g = dr("norm_final_g", [D])
    gmask_d = dr("gmask", [128, 256])
    vmask_d = dr("vmask", [128, 256])
    out_d = dr("out", [NT, D], F32, "ExternalOutput")
    dbg = {}

    st = contextlib.ExitStack()
    with st:
        P = Prog(nc, st)
        sbt = lambda name, shape, dt: st.enter_context(nc.sbuf_tensor(name, shape, dt))

        ident = sbt("ident", [128, 128], BF16); r_ident = P.res("ident")
        tri = sbt("tri", [128, 128], BF16); r_tri = P.res("tri")
        onesb = sbt("onesb", [128, 128], BF16); r_onesb = P.res("onesb")
        onesf = sbt("onesf", [128, 128], F32); r_onesf = P.res("onesf")
        cf = sbt("cf", [128, 128], F32); r_cf = P.res("cf")
        identf = sbt("identf", [128, 128], F32); r_identf = P.res("identf")
        st4 = sbt("st4", [128, 64], F32); r_st4 = P.res("st4")
        gmask = sbt("gmask_s", [128, 16, 16], F32); r_gmask = P.res("gmask")
        vmask = sbt("vmask_s", [128, 16, 16], F32); r_vmask = P.res("vmask")
        cT = sbt("cT", [128, 8], F32); r_cT = P.res("cT")
        cact = sbt("cact", [128, 8], F32); r_cact = P.res("cact")
        modT = sbt("modT", [128, 48], F32); r_modT = P.res("modT")
        gvec = sbt("gvec", [128, 4, 8], F32); r_gvec = P.res("gvec")
        AB = sbt("AB", [128, 4, 8], F32); r_AB = P.res("AB")
        Ct = sbt("Ct", [128, 8, 128], F32); r_Ct = P.res("Ct")
        wsT = sbt("wsT", [128, 8, 128], BF16); r_wsT = P.res("wsT")
        ss = sbt("ss", [128, 64], F32); r_ss = P.res("ss")
        sq = sbt("sq", [128, 64], F32); r_sq = P.res("sq")
        junk = sbt("junk", [128, 1024], BF16); r_junk = P.res("junk")
        junk2 = sbt("junk2", [128, 1024], BF16); r_junk2 = P.res("junk2")

        ARENA_KB = 192
        arena = sbt("arena", [128, ARENA_KB * 512], BF16)

        def view(off_kb, shape, dt):
            n = int(np.prod(shape[1:]))
            nb = n * (4 if dt == F32 else 2)
            o = int(round(off_kb * 512))
            assert o * 2 == int(round(off_kb * 1024)), off_kb
            assert o * 2 + nb <= ARENA_KB * 1024, (off_kb, shape)
            a = arena[:, o:o + nb // 2]
            if dt == F32:
                a = a.bitcast(F32)
            if len(shape) == 3:
                a = a.rearrange("p (a b) -> p a b", a=shape[1])
            elif len(shape) == 4:
                a = a.rearrange("p (a b c) -> p a b c", a=shape[1], b=shape[2])
            return a

        pp = [st.enter_context(nc.psum_tensor(f"pp{i}", [128, 2, 512], F32)) for i in range(4)]
        r_bank = [P.res(f"bank{i}") for i in range(8)]
        bank_ap = [pp[i // 2][:, i % 2, :] for i in range(8)]
        bank_bf = [pp[i // 2][:, i % 2, :].bitcast(BF16) for i in range(8)]
        rot = {"b": 0, "p": 0, "o": 0, "nb": 6, "np": 3, "no": 2}

        def set_rot(nb, np_, no):
            rot["nb"], rot["np"], rot["no"] = nb, np_, no
            rot["b"] %= nb; rot["p"] %= np_; rot["o"] %= no

        def next_bank():
            i = rot["b"]; rot["b"] = (i + 1) % rot["nb"]
            return i

        def next_pair():
            i = rot["p"]; rot["p"] = (i + 1) % rot["np"]
            return i

        def next_obank():
            i = rot["o"]; rot["o"] = (i + 1) % rot["no"]
            return 8 - rot["no"] + i

        def MM(out, lhsT, rhs, start, stop, reads, writes, inc):
            return P.op("pe", lambda e: e.matmul(out, lhsT=lhsT, rhs=rhs, start=start, stop=stop),
                        reads=reads, writes=writes, inc=inc)

        def TR(out, in_, reads, writes, inc):
            return P.op("pe", lambda e: e.transpose(out=out, in_=in_, identity=ident[:]),
                        reads=list(reads) + [r_ident], writes=writes, inc=inc)

        def ACT(out, in_, func, reads, writes, scale=1.0, bias=0.0, accum_out=None):
            if accum_out is None:
                return P.op("act", lambda e: e.activation(out=out, in_=in_, func=func, scale=scale, bias=bias),
                            reads=reads, writes=writes)
            return P.op("act", lambda e: e.activation(out=out, in_=in_, func=func, scale=scale, bias=bias,
                                                      accum_out=accum_out), reads=reads, writes=writes)

        def TS(eng, out, in0, s1, s2, op0, op1, reads, writes):
            if s2 is None:
                return P.op(eng, lambda e: e.tensor_scalar(out=out, in0=in0, scalar1=s1, scalar2=None, op0=op0),
                            reads=reads, writes=writes)
            return P.op(eng, lambda e: e.tensor_scalar(out=out, in0=in0, scalar1=s1, scalar2=s2, op0=op0, op1=op1),
                        reads=reads, writes=writes)

        def TT(eng, out, in0, in1, op, reads, writes):
            return P.op(eng, lambda e: e.tensor_tensor(out=out, in0=in0, in1=in1, op=op), reads=reads, writes=writes)

        def STT(eng, out, in0, scalar, in1, op0, op1, reads, writes):
            return P.op(eng, lambda e: e.scalar_tensor_tensor(out=out, in0=in0, scalar=scalar, in1=in1, op0=op0, op1=op1),
                        reads=reads, writes=writes)

        def CP(eng, out, in_, reads, writes):
            return P.op(eng, lambda e: e.tensor_copy(out=out, in_=in_), reads=reads, writes=writes)

        def MS(eng, ap, val, writes):
            return P.op(eng, lambda e: e.memset(ap, val), writes=writes)

        def wload(dst, src, res, eng="pool"):
            return P.dma(eng, dst, src, writes=[res])

        out_toks = []

        def dump(name, ap, shape, dt, res_list):
            if not debug:
                return
            d = nc.dram_tensor("dbg_" + name, shape, dt, kind="ExternalOutput").ap()
            dbg[name] = d
            r = P.res("dbg_" + name)
            out_toks.append(P.dma("sp", d, ap, reads=list(res_list) + [r]))

        MS("pool", cf[:], 0.0, [r_cf])
        P.op("pool", lambda e: e.affine_select(out=cf[:], in_=cf[:], pattern=[[-1, 128]], compare_op=ALU.not_equal,
                                               fill=1.0, base=0, channel_multiplier=1), reads=[r_cf], writes=[r_cf])
        CP("pool", ident[:], cf[:], [r_cf], [r_ident])
        CP("pool", identf[:], cf[:], [r_cf], [r_identf])
        MS("pool", onesf[:], 1.0, [r_onesf])
        CP("pool", onesb[:], onesf[:], [r_onesf], [r_onesb])
        P.op("pool", lambda e: e.affine_select(out=cf[:], in_=onesf[:], pattern=[[1, 128]], compare_op=ALU.is_ge,
                                               fill=0.0, base=0, channel_multiplier=-1), reads=[r_onesf], writes=[r_cf])
        CP("pool", tri[:], cf[:], [r_cf], [r_tri])

        P.dma("sp", gmask[:], gmask_d.rearrange("p (a b) -> p a b", a=16), writes=[r_gmask])
        P.dma("sp", vmask[:], vmask_d.rearrange("p (a b) -> p a b", a=16), writes=[r_vmask])
        P.dma("sp", cT[:], c_row.rearrange("(k p) -> p k", p=128), writes=[r_cT], allow_slow_non_contiguous=True)
        for i, v in enumerate((norm_mix_g, norm_ffn_g, ln_v_g, ln_v_b)):
            P.dma("sp", gvec[:, i, :], v.rearrange("(k p) -> p k", p=128), writes=[r_gvec],
                  allow_slow_non_contiguous=True)
        ACT(cact[:], cT[:], AF.Silu, [r_cT], [r_cact])

        wada_v = w_ada.rearrange("(k p) n -> p k n", p=128)
        wst = [view(108 + 8 * i, [128, 8, 512], BF16) for i in range(4)]
        r_wst = [P.res(f"wst{i}") for i in range(4)]
        cactb = sbt("cactb", [128, 8], BF16); r_cactb = P.res("cactb")
        CP("dve", cactb[:], cact[:], [r_cact], [r_cactb])
        modrow = view(140, [128, 6144], F32)
        r_modrow = P.res("modrow")
        r_bada = P.res("bada")
        bada_row = view(164, [128, 6144], F32)
        P.dma("sp", bada_row[0:1, :], b_ada.rearrange("(o n) -> o n", o=1), writes=[r_bada])

        mod_piece_state = {"j": 0}

        def mod_piece():
            j = mod_piece_state["j"]
            if j >= 12:
                return False
            mod_piece_state["j"] = j + 1
            s = j % 4
            P.dma("pool", wst[s], wada_v[:, :, j * 512:(j + 1) * 512], writes=[r_wst[s]])
            b = next_bank()
            for k in range(8):
                MM(bank_ap[b][0:1, :], cactb[:, k:k + 1], wst[s][:, k, :], k == 0, k == 7,
                   [r_cactb, r_wst[s]], [r_bank[b]], k == 7)
            TT("dve", modrow[0:1, j * 512:(j + 1) * 512], bank_ap[b][0:1, :], bada_row[0:1, j * 512:(j + 1) * 512], ALU.add,
               [r_bank[b], r_bada], [r_modrow])
            return True

        def mod_transpose(j0, j1):
            b = next_bank()
            for j in range(j0, j1):
                MM(bank_ap[b][:, j:j + 1], modrow[0:1, j * 128:(j + 1) * 128], onesf[0:1, 0:1], True, True,
                   [r_modrow, r_onesf], [r_bank[b]], j == j1 - 1)
            CP("dve", modT[:, j0:j1], bank_ap[b][:, j0:j1], [r_bank[b]], [r_modT])

        for _ in range(4):
            mod_piece()
        mod_transpose(0, 16)
        STT("dve", AB[:, 0, :], modT[:, 8:16], 1.0, gvec[:, 0, :], ALU.add, ALU.mult, [r_modT, r_gvec], [r_AB])
        CP("dve", AB[:, 1, :], modT[:, 0:8], [r_modT], [r_AB])

        if debug:
            dump("AB", AB[:], [128, 4, 8], F32, [r_AB])

        hT_own = view(0, [128, 8, NT], BF16)
        hT_ctx = view(32, [128, 8, NT], BF16)
        r_hown = [P.res(f"hown{g}") for g in range(4)]
        r_hctx = [P.res(f"hctx{g}") for g in range(4)]
        xsl = [view(76 + 4 * i, [128, 1024], F32) for i in range(8)]
        r_xsl = [P.res(f"xsl{i}") for i in range(8)]
        xnb = [view(64 + 2 * i, [128, 1024], BF16) for i in range(2)]
        r_xnb = [P.res(f"xnb{i}") for i in range(2)]

        def norm_to_hT(src_dram, dstT, r_dst, scol, A, Bv, interleave=None):
            for g in range(2):
                MS("dve", ss[:, scol + g * 8: scol + g * 8 + 8], 0.0, [r_ss])
                for i in range(8):
                    t = g * 8 + i
                    P.dma("sp", xsl[i], src_dram[t * 128:(t + 1) * 128, :], writes=[r_xsl[i]])
                    if t % 2 == 0:
                        ACT(junk[:], xsl[i], AF.Square, [r_xsl[i]], [r_junk, r_ss],
                            accum_out=ss[:, scol + t: scol + t + 1])
                    else:
                        P.op("dve", lambda e, i=i, t=t: e.scalar_tensor_tensor(
                            out=junk2[:], in0=xsl[i], scalar=1.0, in1=xsl[i], op0=ALU.mult, op1=ALU.mult,
                            accum_out=ss[:, scol + t: scol + t + 1]), reads=[r_xsl[i]], writes=[r_junk2, r_ss])
                c0 = scol + g * 8
                TS("dve", sq[:, c0:c0 + 8], ss[:, c0:c0 + 8], 1.0 / D, EPS, ALU.mult, ALU.add, [r_ss], [r_sq])
                ACT(sq[:, c0:c0 + 8], sq[:, c0:c0 + 8], AF.Sqrt, [r_sq], [r_sq])
                P.op("dve", lambda e, c0=c0: e.reciprocal(out=sq[:, c0:c0 + 8], in_=sq[:, c0:c0 + 8]),
                     reads=[r_sq], writes=[r_sq])
                for i in range(8):
                    t = g * 8 + i
                    xb = t % 2
                    TS("dve", xnb[xb], xsl[i], sq[:, scol + t: scol + t + 1], None, ALU.mult, None,
                       [r_xsl[i], r_sq], [r_xnb[xb]])
                    b = next_bank()
                    pv = bank_bf[b].rearrange("p (k n) -> p k n", k=8)
                    for k in range(8):
                        TR(pv[:, k, :], xnb[xb][:, k * 128:(k + 1) * 128], [r_xnb[xb]], [r_bank[b]], k == 7)
                    for k in range(8):
                        if k < 8:
                            ACT(dstT[:, k, t * 128:(t + 1) * 128], pv[:, k, :], AF.Identity,
                                [r_bank[b], r_AB], [r_dst[t // 4]], scale=A[:, k:k + 1], bias=Bv[:, k:k + 1])
                        else:
                            TS("dve", dstT[:, k, t * 128:(t + 1) * 128], pv[:, k, :], A[:, k:k + 1], Bv[:, k:k + 1],
                               ALU.mult, ALU.add, [r_bank[b], r_AB], [r_dst[t // 4]])
                    if interleave is not None and t % 2 == 1:
                        interleave()

        if upto >= 1:
            norm_to_hT(x_own, hT_own, r_hown, 0, AB[:, 0, :], AB[:, 1, :], interleave=mod_piece)
            norm_to_hT(x_ctx, hT_ctx, r_hctx, 16, AB[:, 0, :], AB[:, 1, :], interleave=mod_piece)
            while mod_piece():
                pass
            mod_transpose(16, 48)
            STT("dve", AB[:, 2, :], modT[:, 32:40], 1.0, gvec[:, 1, :], ALU.add, ALU.mult, [r_modT, r_gvec], [r_AB])
            CP("dve", AB[:, 3, :], modT[:, 24:32], [r_modT], [r_AB])
            if debug and upto == 1:
                dump("modT", modT[:], [128, 48], F32, [r_modT])
                dump("hT_own", hT_own, [128, 8, NT], BF16, r_hown)
                dump("hT_ctx", hT_ctx, [128, 8, NT], BF16, r_hctx)


        def viewb(off_b, shape, dt):
            return view(off_b / 1024.0, shape, dt)

        KB = 1024
        ybT = view(76, [128, 8, NT], BF16)
        if upto >= 2:
            F2 = P.fence()
            r_ybT = [P.res(f"ybT{h}", after=F2) for h in range(8)]
            Vp = view(108, [128, 32, 4, 129], BF16); r_V = P.res("V", after=F2)
            qT = viewb(141 * KB, [128, NT], BF16); r_qT = P.res("qT", after=F2)
            kT = viewb(145 * KB, [128, 4096], BF16); r_kT = P.res("kT", after=F2)
            Oacc = viewb(153 * KB, [128, 16, 129], F32)
            r_Oacc = [P.res(f"Oacc{i}", after=F2) for i in range(16)]
            Pbuf = [viewb(161 * KB + 256 + 2048 * i, [128, 2, 512], BF16) for i in range(3)]
            r_Pbuf = [P.res(f"Pbuf{i}", after=F2) for i in range(3)]
            gm = viewb(167 * KB + 256, [128, 16, 16], F32); r_gm = P.res("gm", after=F2)
            sel = viewb(168 * KB + 256, [128, 16, 16], F32); r_sel = P.res("sel", after=F2)
            mx8 = viewb(169 * KB + 256, [128, 16, 8], F32); r_mx8 = P.res("mx8", after=F2)
            ksum = viewb(169 * KB + 768, [128, 16], F32); r_ksum = P.res("ksum", after=F2)
            kbar = viewb(169 * KB + 832, [128, 16], BF16); r_kbar = P.res("kbar", after=F2)
            rec = viewb(169 * KB + 896, [128, 16, 1], F32); r_rec = P.res("rec", after=F2)
            ynorm = viewb(170 * KB + 256, [128, 16, 128], BF16); r_ynorm = P.res("ynorm", after=F2)
            wq = [[viewb((176 + 8 * s) * KB + 2048 * hp, [128, 8, 128], BF16) for hp in range(2)] for s in range(2)]
            wk = [[viewb((176 + 8 * s) * KB + 4096 + 2048 * hp, [128, 8, 128], BF16) for hp in range(2)] for s in range(2)]
            wv4 = view(64, [128, 8, 512], BF16); r_wv4 = P.res("wv4", after=P.inherit(*r_xnb))
            r_wq = [[P.res(f"wq{s}{hp}", after=P.inherit(r_bada)) for hp in range(2)] for s in range(2)]
            r_wk = [[P.res(f"wk{s}{hp}", after=P.inherit(r_bada)) for hp in range(2)] for s in range(2)]
            win_v = w_in.rearrange("(k p) n -> p k n", p=128)

            def load_pair_weights(p):
                s = p % 2
                for hp in range(2):
                    h = 2 * p + hp
                    wload(wq[s][hp], win_v[:, :, 2048 + h * 128: 2048 + (h + 1) * 128], r_wq[s][hp])
                    wload(wk[s][hp], win_v[:, :, 3072 + h * 128: 3072 + (h + 1) * 128], r_wk[s][hp])

            def load_group_v(g):
                wload(wv4, win_v[:, :, 4096 + g * 512: 4096 + (g + 1) * 512], r_wv4)

            def hsrc(lt):
                if lt < 16:
                    return hT_ctx, r_hctx[lt // 4], lt
                return hT_own, r_hown[(lt - 16) // 4], lt - 16

            MS("pool", Vp[:, :, :, 128:129], 1.0, [r_V])
            load_pair_weights(0)
            load_group_v(0)
            set_rot(4, 2, 4)
            npairs = 4 if upto > 2 or not debug else int(dbg.get("_npairs", 4))
            for p in range(4):
                s = p % 2
                if p % 2 == 0:
                    for lt in range(32):
                        b = next_bank()
                        src, rs, ti = hsrc(lt)
                        for k in range(8):
                            MM(bank_ap[b], src[:, k, ti * 128:(ti + 1) * 128], wv4[:, k, :],
                               k == 0, k == 7, [rs, r_wv4], [r_bank[b]], k == 7)
                        o_ap = Vp[:, lt, :, 0:128]
                        i_ap = bank_ap[b].rearrange("p (h d) -> p h d", h=4)
                        if lt % 2 == 0:
                            CP("dve", o_ap, i_ap, [r_bank[b]], [r_V])
                        else:
                            ACT(o_ap, i_ap, AF.Identity, [r_bank[b]], [r_V])
                    if p == 0:
                        load_group_v(1)
                if debug and upto == 2 and p == 0:
                    dump("V0", Vp, [128, 32, 4, 129], BF16, [r_V])
                for hp in range(2):
                    h = 2 * p + hp
                    for tg in range(4):
                        b = next_bank()
                        for k in range(8):
                            MM(bank_ap[b], wq[s][hp][:, k, :], hT_own[:, k, tg * 512:(tg + 1) * 512], k == 0, k == 7,
                               [r_wq[s][hp], r_hown[tg]], [r_bank[b]], k == 7)
                        TS("dve", qT[:, tg * 512:(tg + 1) * 512], bank_ap[b], QSCALE, None, ALU.mult, None,
                           [r_bank[b]], [r_qT])
                    MS("dve", ksum[:], 0.0, [r_ksum])
                    for tg in range(8):
                        src, rs = (hT_ctx, r_hctx[tg]) if tg < 4 else (hT_own, r_hown[tg - 4])
                        tl = tg % 4
                        b = next_bank()
                        for k in range(8):
                            MM(bank_ap[b], wk[s][hp][:, k, :], src[:, k, tl * 512:(tl + 1) * 512], k == 0, k == 7,
                               [r_wk[s][hp], rs], [r_bank[b]], k == 7)
                        for j in range(2):
                            ACT(kT[:, tg * 512 + j * 256: tg * 512 + (j + 1) * 256], bank_ap[b][:, j * 256:(j + 1) * 256],
                                AF.Identity, [r_bank[b]], [r_kT, r_ksum], accum_out=ksum[:, 2 * tg + j: 2 * tg + j + 1])
                    TS("dve", kbar[:], ksum[:], 1.0 / 256, None, ALU.mult, None, [r_ksum], [r_kbar])
                    if h == 7 and upto >= 3:
                        wpb = view(32, [128, 8, 1024], BF16); r_wpb = P.res("wpb", after=P.inherit(*r_hctx))
                        wgb = view(48, [128, 8, 1024], BF16); r_wgb = P.res("wgb", after=P.inherit(*r_hctx))
                        wload(wpb, w_proj_b.rearrange("(k p) n -> p k n", p=128), r_wpb)
                        wload(wgb, win_v[:, :, 6144:7168], r_wgb)
                    if debug and upto == 2 and p == 0:
                        dump(f"qT{h}", qT, [128, NT], BF16, [r_qT])
                        dump(f"kT{h}", kT, [128, 4096], BF16, [r_kT])
                    b = next_bank()
                    for qt in range(16):
                        MM(bank_ap[b][:, qt * 16:(qt + 1) * 16], qT[:, qt * 128:(qt + 1) * 128], kbar[:], True, True,
                           [r_qT, r_kbar], [r_bank[b]], qt == 15)
                    TT("dve", gm[:], bank_ap[b][:, 0:256].rearrange("p (a b) -> p a b", a=16), gmask[:], ALU.add,
                       [r_bank[b], r_gmask], [r_gm])
                    for qt in range(16):
                        P.op("dve", lambda e, qt=qt: e.max(out=mx8[:, qt, :], in_=gm[:, qt, :]), reads=[r_gm], writes=[r_mx8])
                    for qt in range(16):
                        STT("dve", sel[:, qt, :], gm[:, qt, :], mx8[:, qt, 2:3], vmask[:, qt, :], ALU.is_ge, ALU.mult,
                            [r_gm, r_mx8, r_vmask], [r_sel])
                    if debug and upto == 2 and p == 0 and hp == 0:
                        dump("sel0", sel, [128, 16, 16], F32, [r_sel])
                        dump("gm0", gm, [128, 16, 16], F32, [r_gm])

                    units = []

                    def diag_unit(j):
                        bb = 8 + j
                        qa, qb = 2 * j, 2 * j + 1
                        kt0, kt1 = 2 * bb, 2 * bb + 1
                        stt = {}

                        def A():
                            pr = next_pair()
                            pbi = stt["pb"]
                            pb = Pbuf[pbi]
                            MM(pp[pr][:, 0, 0:256], kT[:, kt0 * 128:(kt0 + 1) * 128], qT[:, qa * 128:(qa + 2) * 128],
                               True, True, [r_kT, r_qT], [r_bank[2 * pr]], False)
                            MM(pp[pr][:, 1, 0:256], kT[:, kt1 * 128:(kt1 + 1) * 128], qT[:, qa * 128:(qa + 2) * 128],
                               True, True, [r_kT, r_qT], [r_bank[2 * pr + 1]], True)
                            ACT(pb[:, :, 0:256], pp[pr][:, :, 0:256], AF.Exp, [r_bank[2 * pr], r_bank[2 * pr + 1]],
                                [r_Pbuf[pbi]])
                            TT("dve", pb[:, 0, 0:128], pb[:, 0, 0:128], tri[:], ALU.mult, [r_Pbuf[pbi], r_tri], [r_Pbuf[pbi]])
                            TT("dve", pb[:, 1, 128:256], pb[:, 1, 128:256], tri[:], ALU.mult, [r_Pbuf[pbi], r_tri],
                               [r_Pbuf[pbi]])

                        def B():
                            pbi = stt["pb"]
                            pb = Pbuf[pbi]
                            ob = next_obank()
                            MM(bank_ap[ob][:, 0:129], pb[:, 0, 0:128], Vp[:, kt0, h % 4, :], True, True,
                               [r_Pbuf[pbi], r_V], [r_bank[ob]], True)
                            ACT(Oacc[:, qa, :], bank_ap[ob][:, 0:129], AF.Identity, [r_bank[ob]], [r_Oacc[qa]])
                            ob = next_obank()
                            MM(bank_ap[ob][:, 0:129], pb[:, 0, 128:256], Vp[:, kt0, h % 4, :], True, False,
                               [r_Pbuf[pbi], r_V], [r_bank[ob]], False)
                            MM(bank_ap[ob][:, 0:129], pb[:, 1, 128:256], Vp[:, kt1, h % 4, :], False, True,
                               [r_Pbuf[pbi], r_V], [r_bank[ob]], True)
                            ACT(Oacc[:, qb, :], bank_ap[ob][:, 0:129], AF.Identity, [r_bank[ob]], [r_Oacc[qb]])
                        return stt, A, B

                    def past_unit(bb, c0, ntile):
                        kt0, kt1 = 2 * bb, 2 * bb + 1
                        ncols = ntile * 128
                        stt = {}

                        def A():
                            pr = next_pair()
                            pbi = stt["pb"]
                            pb = Pbuf[pbi]
                            MM(pp[pr][:, 0, 0:ncols], kT[:, kt0 * 128:(kt0 + 1) * 128], qT[:, c0 * 128: c0 * 128 + ncols],
                               True, True, [r_kT, r_qT], [r_bank[2 * pr]], False)
                            MM(pp[pr][:, 1, 0:ncols], kT[:, kt1 * 128:(kt1 + 1) * 128], qT[:, c0 * 128: c0 * 128 + ncols],
                               True, True, [r_kT, r_qT], [r_bank[2 * pr + 1]], True)
                            ACT(pb[:, :, 0:ncols], pp[pr][:, :, 0:ncols], AF.Exp, [r_bank[2 * pr], r_bank[2 * pr + 1]],
                                [r_Pbuf[pbi]])

                        def B():
                            pbi = stt["pb"]
                            pb = Pbuf[pbi]
                            for i in range(ntile):
                                qt = c0 + i
                                ob = next_obank()
                                MM(bank_ap[ob][:, 0:129], pb[:, 0, i * 128:(i + 1) * 128], Vp[:, kt0, h % 4, :], True, False,
                                   [r_Pbuf[pbi], r_V], [r_bank[ob]], False)
                                MM(bank_ap[ob][:, 0:129], pb[:, 1, i * 128:(i + 1) * 128], Vp[:, kt1, h % 4, :], False, True,
                                   [r_Pbuf[pbi], r_V], [r_bank[ob]], True)
                                STT("dve", Oacc[:, qt, :], bank_ap[ob][:, 0:129], sel[:, qt, bb:bb + 1], Oacc[:, qt, :],
                                    ALU.mult, ALU.add, [r_bank[ob], r_sel, r_Oacc[qt]], [r_Oacc[qt]])
                        return stt, A, B

                    for j in range(8):
                        units.append(diag_unit(j))
                    for bb in range(15):
                        qs = 2 * bb if bb < 8 else 2 * (bb - 8 + 1)
                        for c0 in range(qs, 16, 4):
                            units.append(past_unit(bb, c0, min(4, 16 - c0)))
                    for ui, (stt, A, B) in enumerate(units):
                        stt["pb"] = ui % 3
                    DEPTH = 1
                    for ui in range(len(units) + DEPTH):
                        if ui < len(units):
                            units[ui][1]()
                        if ui >= DEPTH:
                            units[ui - DEPTH][2]()

                    P.op("dve", lambda e: e.reciprocal(out=rec[:], in_=Oacc[:, :, 128:129]), reads=r_Oacc, writes=[r_rec])
                    TT("dve", ynorm[:], Oacc[:, :, 0:128], rec[:].to_broadcast([128, 16, 128]), ALU.mult,
                       list(r_Oacc) + [r_rec], [r_ynorm])
                    for hf in range(2):
                        b = next_bank()
                        pv = bank_bf[b].rearrange("p (k n) -> p k n", k=8)
                        for i in range(8):
                            TR(pv[:, i, :], ynorm[:, hf * 8 + i, :], [r_ynorm], [r_bank[b]], i == 7)
                        if hf == 0:
                            CP("dve", ybT[:, h, hf * 1024:(hf + 1) * 1024], bank_bf[b], [r_bank[b]], [r_ybT[h]])
                        else:
                            ACT(ybT[:, h, hf * 1024:(hf + 1) * 1024], bank_bf[b], AF.Identity, [r_bank[b]], [r_ybT[h]])
                    if hp == 0 and p + 1 < 4:
                        load_pair_weights(p + 1)
            set_rot(8, 4, 2)
            if debug and upto == 2:
                dump("ybT", ybT, [128, 8, NT], BF16, r_ybT)


        def bcast_row_from_modT(dst, col0, res_dst):
            pr = next_pair()
            for c in range(8):
                TS("dve", cf[:], identf[:], modT[:, col0 + c: col0 + c + 1], None, ALU.mult, None,
                   [r_identf, r_modT], [r_cf])
                MM(pp[pr][:, c // 4, (c % 4) * 128:(c % 4 + 1) * 128], onesf[:], cf[:], True, True,
                   [r_onesf, r_cf], [r_bank[2 * pr + c // 4]], True)
            CP("dve", dst, pp[pr][:, :, :].rearrange("p a b -> p (a b)"), [r_bank[2 * pr], r_bank[2 * pr + 1]], [res_dst])

        merged = view(108, [128, 8, NT], BF16)
        if upto >= 3:
            F3 = P.fence()
            r_merged = [P.res(f"merged{g}", after=F3) for g in range(4)]
            sgt = [view(72 + 2 * i, [128, 512], F32) for i in range(2)]
            r_sgt = [P.res(f"sgt{i}", after=F3) for i in range(2)]
            wu_a = view(140, [128, 8, 1024], BF16); r_wu_a = P.res("wu_a", after=F3)
            wvg = view(156, [128, 8, 1024], BF16); r_wvg = P.res("wvg", after=F3)
            wpa = view(172, [128, 8, 1024], BF16); r_wpa = P.res("wpa", after=F3)
            wload(wu_a, win_v[:, :, 0:1024], r_wu_a)
            wload(wvg, win_v[:, :, 1024:2048], r_wvg)
            wload(wpa, w_proj_a.rearrange("(k p) n -> p k n", p=128), r_wpa)
            wsf = view(188, [128, 8, 128], F32); r_wsf = P.res("wsf", after=F3)
            wsb = view(64, [128, 8, 128], BF16); r_wsb = P.res("wsb", after=F3)
            bsb = view(68, [128, 8, 128], F32); r_bsb = P.res("bsb", after=F3)
            P.dma("sp", wsf, w_spatial.rearrange("g t s -> t g s"), writes=[r_wsf])
            P.dma("sp", bsb.rearrange("p a b -> p (a b)"), b_spatial.rearrange("g t -> (g t)").partition_broadcast(128),
                  writes=[r_bsb])
            P.op("pool", lambda e: e.affine_select(out=wsf, in_=wsf, pattern=[[0, 8], [-1, 128]], compare_op=ALU.is_ge,
                                                   fill=0.0, base=0, channel_multiplier=1), reads=[r_wsf], writes=[r_wsf])
            CP("pool", wsb, wsf, [r_wsf], [r_wsb])
            b = next_bank()
            pv = bank_bf[b].rearrange("p (k n) -> p k n", k=8)
            for g in range(8):
                TR(pv[:, g, :], wsb[:, g, :], [r_wsb], [r_bank[b]], g == 7)
            CP("dve", wsT[:], pv, [r_bank[b]], [r_wsT])
            pr = next_pair()
            for g in range(8):
                MM(pp[pr][:, g // 4, (g % 4) * 128:(g % 4 + 1) * 128], onesb[:], wsT[:, g, :], True, True,
                   [r_onesb, r_wsT], [r_bank[2 * pr + g // 4]], True)
            for g in range(8):
                STT("dve", Ct[:, g, :], pp[pr][:, g // 4, (g % 4) * 128:(g % 4 + 1) * 128], gvec[:, 3, g:g + 1], bsb[:, g, :],
                    ALU.mult, ALU.add, [r_bank[2 * pr + g // 4], r_gvec, r_bsb], [r_Ct])

            n = 0
            for tg in range(4):
                for c in range(8):
                    bp = next_bank(); bg = next_bank()
                    for k in range(8):
                        MM(bank_ap[bp], wpb[:, k, c * 128:(c + 1) * 128], ybT[:, k, tg * 512:(tg + 1) * 512], k == 0, k == 7,
                           [r_wpb, r_ybT[k]], [r_bank[bp]], k == 7)
                    for k in range(8):
                        MM(bank_ap[bg], wgb[:, k, c * 128:(c + 1) * 128], hT_own[:, k, tg * 512:(tg + 1) * 512], k == 0, k == 7,
                           [r_wgb, r_hown[tg]], [r_bank[bg]], k == 7)
                    si = n % 2; n += 1
                    ACT(sgt[si], bank_ap[bg], AF.Sigmoid, [r_bank[bg]], [r_sgt[si]])
                    TT("dve", merged[:, c, tg * 512:(tg + 1) * 512], sgt[si], bank_ap[bp], ALU.mult,
                       [r_sgt[si], r_bank[bp]], [r_merged[tg]])
            if debug and upto == 3:
                dump("merged", merged, [128, 8, NT], BF16, r_merged)

        if upto >= 4:
            F4 = P.fence()
            wga = view(76, [128, 8, 1024], BF16); r_wga = P.res("wga", after=F4)
            wload(wga, win_v[:, :, 5120:6144], r_wga)
            uT = view(92, [128, 8, 512], BF16); r_uT = P.res("uT", after=F4)
            yaT = view(100, [128, 8, 512], BF16); r_yaT = P.res("yaT", after=F4)
            gv = [view(44 + 4 * i, [128, 1024], F32) for i in range(4)]
            r_gv = [P.res(f"gv{i}", after=F4) for i in range(4)]
            tmpM = [view(32 + 4 * i, [128, 8, 128], F32) for i in range(2)]
            r_tmpM = [P.res(f"tmpM{i}", after=F4) for i in range(2)]
            sg2 = [view(40 + 2 * i, [128, 512], F32) for i in range(2)]
            r_sg2 = [P.res(f"sg2{i}", after=F4) for i in range(2)]

            F4b = P.fence()
            uTs = [uT, view(68, [128, 8, 512], BF16)]
            r_uTs = [r_uT, P.res("uT1", after=F4b)]
            vhat = [view(60 + 2 * i, [128, 1024], BF16) for i in range(4)]
            r_vhat = [P.res(f"vhat{i}", after=F4b) for i in range(4)]
            junkf = junk2[:].bitcast(F32)
            r_junkf = r_junk2
            stb = [st4, cf]
            r_st8 = [r_st4, r_cf]
            cnt4 = {"n": 0}

            def U(tg, cs=range(8)):
                ut, rut = uTs[tg % 2], r_uTs[tg % 2]
                for c in cs:
                    b = next_bank()
                    for k in range(8):
                        MM(bank_ap[b], wu_a[:, k, c * 128:(c + 1) * 128], hT_own[:, k, tg * 512:(tg + 1) * 512], k == 0, k == 7,
                           [r_wu_a, r_hown[tg]], [r_bank[b]], k == 7)
                    ACT(ut[:, c, :], bank_ap[b], AF.Gelu_apprx_tanh, [r_bank[b]], [rut])

            def V(tg, tiles=range(4), init=True):
                o = 0; rs = r_st8[tg % 2]; st8 = stb[tg % 2]
                if init:
                    MS("dve", st8[:, o:o + 16], 0.0, [rs])
                for i in tiles:
                    t = 4 * tg + i
                    pr = next_pair()
                    for hf in range(2):
                        for k in range(8):
                            MM(pp[pr][:, hf, :], hT_own[:, k, t * 128:(t + 1) * 128], wvg[:, k, hf * 512:(hf + 1) * 512],
                               k == 0, k == 7, [r_hown[tg], r_wvg], [r_bank[2 * pr + hf]], k == 7)
                        ACT(gv[i][:, hf * 512:(hf + 1) * 512], pp[pr][:, hf, :], AF.Gelu_apprx_tanh, [r_bank[2 * pr + hf]],
                            [r_gv[i], rs], accum_out=st8[:, o + 2 * i + hf: o + 2 * i + hf + 1])
                    for hf in range(2):
                        P.op("dve", lambda e, i=i, hf=hf, o=o, st8=st8: e.scalar_tensor_tensor(
                            out=junkf, in0=gv[i][:, hf * 512:(hf + 1) * 512], scalar=1.0, in1=gv[i][:, hf * 512:(hf + 1) * 512],
                            op0=ALU.mult, op1=ALU.mult, accum_out=st8[:, o + 8 + 2 * i + hf: o + 8 + 2 * i + hf + 1]),
                            reads=[r_gv[i]], writes=[r_junkf, rs])

            def STATS_VH(tg):
                o = 0; rs = r_st8[tg % 2]; st8 = stb[tg % 2]
                s1 = st8[:, o:o + 8].rearrange("p (i h) -> p i h", h=2)
                s2 = st8[:, o + 8:o + 16].rearrange("p (i h) -> p i h", h=2)
                c = lambda a: st8[:, o + a:o + a + 4]
                TT("dve", c(16), s1[:, :, 0], s1[:, :, 1], ALU.add, [rs], [rs])
                TT("dve", c(20), s2[:, :, 0], s2[:, :, 1], ALU.add, [rs], [rs])
                TS("dve", c(24), c(16), 1.0 / D, None, ALU.mult, None, [rs], [rs])
                TT("dve", c(28), c(24), c(24), ALU.mult, [rs], [rs])
                STT("dve", c(32), c(20), 1.0 / D, c(28), ALU.mult, ALU.subtract, [rs], [rs])
                TS("dve", c(32), c(32), EPS, None, ALU.add, None, [rs], [rs])
                ACT(c(36), c(32), AF.Sqrt, [rs], [rs])
                P.op("dve", lambda e: e.reciprocal(out=c(40), in_=c(36)), reads=[rs], writes=[rs])
                STT("dve", c(44), c(24), -1.0, c(40), ALU.mult, ALU.mult, [rs], [rs])
                for i in range(4):
                    TS("dve", vhat[i], gv[i], st8[:, o + 40 + i:o + 41 + i], st8[:, o + 44 + i:o + 45 + i], ALU.mult, ALU.add,
                       [r_gv[i], rs], [r_vhat[i]])

            def SP(tg, tiles=range(4)):
                ut, rut = uTs[tg % 2], r_uTs[tg % 2]
                for i in tiles:
                    vb = i % 2
                    pr = next_pair()
                    for g in range(8):
                        MM(pp[pr][:, g // 4, (g % 4) * 128:(g % 4 + 1) * 128], vhat[i][:, g * 128:(g + 1) * 128], wsT[:, g, :],
                           True, True, [r_vhat[i], r_wsT], [r_bank[2 * pr + g // 4]], g == 3 or g == 7)
                    mt = tmpM[vb]
                    TT("dve", mt, pp[pr][:, :, :].rearrange("p a (b c) -> p (a b) c", b=4),
                       gvec[:, 2, :].unsqueeze(2).to_broadcast([128, 8, 128]), ALU.mult,
                       [r_bank[2 * pr], r_bank[2 * pr + 1], r_gvec], [r_tmpM[vb]])
                    TT("dve", mt, mt, Ct[:], ALU.add, [r_tmpM[vb], r_Ct], [r_tmpM[vb]])
                    TT("dve", yaT[:, :, i * 128:(i + 1) * 128], mt, ut[:, :, i * 128:(i + 1) * 128], ALU.mult,
                       [r_tmpM[vb], rut], [r_yaT])

            def PAGA(tg):
                for c in range(8):
                    bp = next_bank(); bg = next_bank()
                    for k in range(8):
                        MM(bank_ap[bp], wpa[:, k, c * 128:(c + 1) * 128], yaT[:, k, :], k == 0, k == 7,
                           [r_wpa, r_yaT], [r_bank[bp]], k == 7)
                    for k in range(8):
                        MM(bank_ap[bg], wga[:, k, c * 128:(c + 1) * 128], hT_own[:, k, tg * 512:(tg + 1) * 512], k == 0, k == 7,
                           [r_wga, r_hown[tg]], [r_bank[bg]], k == 7)
                    si = cnt4["n"] % 2; cnt4["n"] += 1
                    ACT(sg2[si], bank_ap[bg], AF.Sigmoid, [r_bank[bg]], [r_sg2[si]])
                    TT("dve", sg2[si], sg2[si], bank_ap[bp], ALU.mult, [r_sg2[si], r_bank[bp]], [r_sg2[si]])
                    TT("dve", merged[:, c, tg * 512:(tg + 1) * 512], merged[:, c, tg * 512:(tg + 1) * 512], sg2[si], ALU.add,
                       [r_merged[tg], r_sg2[si]], [r_merged[tg]])

            U(0); V(0); STATS_VH(0)
            for tg in range(4):
                for i in range(4):
                    SP(tg, [i])
                    if tg + 1 < 4:
                        U(tg + 1, [2 * i, 2 * i + 1])
                        V(tg + 1, [i], init=(i == 0))
                if tg == 2 and upto >= 5:
                    wo = view(140, [128, 8, 1024], BF16); r_wo = P.res("wo", after=P.inherit(r_wu_a))
                    wload(wo, w_out.rearrange("(k p) n -> p k n", p=128), r_wo)
                PAGA(tg)
                if tg + 1 < 4:
                    STATS_VH(tg + 1)
            if debug and upto == 4:
                dump("merged", merged, [128, 8, NT], BF16, r_merged)

        x1 = view(44, [128, 16, 1024], F32)
        if upto >= 5:
            F5 = P.fence()
            r_x1 = [P.res(f"x1_{t}", after=F5) for t in range(16)]
            xs = [view(156 + 4 * i, [128, 1024], F32) for i in range(3)]
            r_xs = [P.res(f"xs{i}", after=F5) for i in range(3)]
            gmb = view(168, [128, 1024], F32); r_gmb = P.res("gmb", after=F5)
            bcast_row_from_modT(gmb, 16, r_gmb)
            MS("dve", ss[:, 32:48], 0.0, [r_ss])
            for t in range(16):
                pr = next_pair()
                for hf in range(2):
                    for k in range(8):
                        MM(pp[pr][:, hf, :], merged[:, k, t * 128:(t + 1) * 128], wo[:, k, hf * 512:(hf + 1) * 512],
                           k == 0, k == 7, [r_merged[t // 4], r_wo], [r_bank[2 * pr + hf]], k == 7)
                xi = t % 3
                P.dma("sp", xs[xi], x_own[t * 128:(t + 1) * 128, :], writes=[r_xs[xi]])
                TT("dve", x1[:, t, :], pp[pr][:, :, :].rearrange("p a b -> p (a b)"), gmb, ALU.mult,
                   [r_bank[2 * pr], r_bank[2 * pr + 1], r_gmb], [r_x1[t]])
                TT("dve", x1[:, t, :], x1[:, t, :], xs[xi], ALU.add, [r_x1[t], r_xs[xi]], [r_x1[t]])
                if t % 2 == 0:
                    ACT(junk[:], x1[:, t, :], AF.Square, [r_x1[t]], [r_junk, r_ss], accum_out=ss[:, 32 + t:33 + t])
                else:
                    P.op("dve", lambda e, t=t: e.scalar_tensor_tensor(
                        out=junk2[:], in0=x1[:, t, :], scalar=1.0, in1=x1[:, t, :], op0=ALU.mult, op1=ALU.mult,
                        accum_out=ss[:, 32 + t:33 + t]), reads=[r_x1[t]], writes=[r_junk2, r_ss])
            if debug and upto == 5:
                dump("x1", x1, [128, 16, 1024], F32, r_x1)

        h2T = hT_own
        if upto >= 6:
            xn2 = [view(172 + 2 * i, [128, 1024], BF16) for i in range(2)]
            r_xn2 = [P.res(f"xn2{i}", after=F5) for i in range(2)]
            TS("dve", sq[:, 32:48], ss[:, 32:48], 1.0 / D, EPS, ALU.mult, ALU.add, [r_ss], [r_sq])
            ACT(sq[:, 32:48], sq[:, 32:48], AF.Sqrt, [r_sq], [r_sq])
            P.op("dve", lambda e: e.reciprocal(out=sq[:, 32:48], in_=sq[:, 32:48]), reads=[r_sq], writes=[r_sq])
            for t in range(16):
                xb = t % 2
                TS("dve", xn2[xb], x1[:, t, :], sq[:, 32 + t:33 + t], None, ALU.mult, None, [r_x1[t], r_sq], [r_xn2[xb]])
                b = next_bank()
                pv = bank_bf[b].rearrange("p (k n) -> p k n", k=8)
                for k in range(8):
                    TR(pv[:, k, :], xn2[xb][:, k * 128:(k + 1) * 128], [r_xn2[xb]], [r_bank[b]], k == 7)
                for k in range(8):
                    if k < 8:
                        ACT(h2T[:, k, t * 128:(t + 1) * 128], pv[:, k, :], AF.Identity, [r_bank[b], r_AB], [r_hown[t // 4]],
                            scale=AB[:, 2, k:k + 1], bias=AB[:, 3, k:k + 1])
                    else:
                        TS("dve", h2T[:, k, t * 128:(t + 1) * 128], pv[:, k, :], AB[:, 2, k:k + 1], AB[:, 3, k:k + 1],
                           ALU.mult, ALU.add, [r_bank[b], r_AB], [r_hown[t // 4]])
            if debug and upto == 6:
                dump("h2T", h2T, [128, 8, NT], BF16, r_hown)

        if upto >= 7:
            F7 = P.fence()
            aT = view(108, [128, 11, NT], BF16)
            r_aT = [P.res(f"aT{g}", after=F7) for g in range(4)]
            wd = view(152, [128, 11, 1024], BF16); r_wd = P.res("wd", after=F7)
            wgs = [view(174 + 4 * i, [128, 8, 128], BF16) for i in range(4)]
            wus = [view(176 + 4 * i, [128, 8, 128], BF16) for i in range(4)]
            r_wgs = [P.res(f"wgs{i}", after=F7) for i in range(4)]
            r_wus = [P.res(f"wus{i}", after=F7) for i in range(4)]
            gfb = view(32, [128, 1024], F32); r_gfb = P.res("gfb", after=F7)
            slt = [view(36 + 2 * i, [128, 512], F32) for i in range(2)]
            r_slt = [P.res(f"slt{i}", after=F7) for i in range(2)]
            tmpd = view(40, [128, 1024], F32); r_tmpd = P.res("tmpd", after=F7)
            wg_v = w_ffn_gate.rearrange("(k p) n -> p k n", p=128)
            wu_v = w_ffn_up.rearrange("(k p) n -> p k n", p=128)
            wd_v = w_ffn_down.rearrange("(c p) n -> p c n", p=128)

            def load_gu(cc):
                sl = cc % 4
                wload(wgs[sl], wg_v[:, :, cc * 128:(cc + 1) * 128], r_wgs[sl])
                wload(wus[sl], wu_v[:, :, cc * 128:(cc + 1) * 128], r_wus[sl])

            bcast_row_from_modT(gfb, 40, r_gfb)
            load_gu(0); load_gu(1)
            n = 0
            for hh in range(2):
                wload(wd, wd_v[:, hh * 11:(hh + 1) * 11, :], r_wd)
                for c in range(11):
                    cc = hh * 11 + c
                    if cc + 2 < NFC:
                        load_gu(cc + 2)
                    sl = cc % 4
                    for tg in range(4):
                        bg = next_bank(); bu = next_bank()
                        for k in range(8):
                            MM(bank_ap[bg], wgs[sl][:, k, :], h2T[:, k, tg * 512:(tg + 1) * 512], k == 0, k == 7,
                               [r_wgs[sl], r_hown[tg]], [r_bank[bg]], k == 7)
                        for k in range(8):
                            MM(bank_ap[bu], wus[sl][:, k, :], h2T[:, k, tg * 512:(tg + 1) * 512], k == 0, k == 7,
                               [r_wus[sl], r_hown[tg]], [r_bank[bu]], k == 7)
                        si = n % 2; n += 1
                        ACT(slt[si], bank_ap[bg], AF.Silu, [r_bank[bg]], [r_slt[si]])
                        TT("dve", aT[:, c, tg * 512:(tg + 1) * 512], slt[si], bank_ap[bu], ALU.mult,
                           [r_slt[si], r_bank[bu]], [r_aT[tg]])
                if hh == 1 and upto >= 8:
                    F8 = P.fence()
                    gfin = view(174, [128, 1024], F32); r_gfin = P.res("gfin", after=F8)
                    ost = [view(178 + 4 * i, [128, 1024], F32) for i in range(3)]
                    r_ost = [P.res(f"ost{i}", after=F8) for i in range(3)]
                    P.dma("sp", gfin, norm_final_g.partition_broadcast(128), writes=[r_gfin])
                    MS("dve", ss[:, 48:64], 0.0, [r_ss])

                    r_fs = [P.res(f"fs{t}") for t in range(16)]

                    def final_a(t):
                        ACT(junk[:], x1[:, t, :], AF.Square, [r_x1[t]], [r_junk, r_fs[t]], accum_out=ss[:, 48 + t:49 + t])

                    def final_b(t):
                        TS("dve", sq[:, 48 + t:49 + t], ss[:, 48 + t:49 + t], 1.0 / D, EPS, ALU.mult, ALU.add, [r_fs[t]], [r_fs[t]])
                        ACT(sq[:, 48 + t:49 + t], sq[:, 48 + t:49 + t], AF.Sqrt, [r_fs[t]], [r_fs[t]])
                        P.op("dve", lambda e, t=t: e.reciprocal(out=sq[:, 48 + t:49 + t], in_=sq[:, 48 + t:49 + t]),
                             reads=[r_fs[t]], writes=[r_fs[t]])
                        oi = t % 3
                        STT("dve", ost[oi], x1[:, t, :], sq[:, 48 + t:49 + t], gfin, ALU.mult, ALU.mult,
                            [r_x1[t], r_fs[t], r_gfin], [r_ost[oi]])
                        out_toks.append(P.dma("sp", out_d[t * 128:(t + 1) * 128, :], ost[oi], reads=[r_ost[oi]]))
                for t in range(16):
                    pr = next_pair()
                    for hf in range(2):
                        for c in range(11):
                            MM(pp[pr][:, hf, :], aT[:, c, t * 128:(t + 1) * 128], wd[:, c, hf * 512:(hf + 1) * 512],
                               c == 0, c == 10, [r_aT[t // 4], r_wd], [r_bank[2 * pr + hf]], c == 10)
                    TT("dve", tmpd, pp[pr][:, :, :].rearrange("p a b -> p (a b)"), gfb, ALU.mult,
                       [r_bank[2 * pr], r_bank[2 * pr + 1], r_gfb], [r_tmpd])
                    TT("dve", x1[:, t, :], x1[:, t, :], tmpd, ALU.add, [r_x1[t], r_tmpd], [r_x1[t]])
                    if hh == 1 and upto >= 8:
                        final_a(t)
                        if t >= 1:
                            final_b(t - 1)
                        if t == 15:
                            final_b(15)

        for t in out_toks:
            P.wait("sp", t)
        P.emit_all()
    return nc, dbg


def make_in_maps(inputs):
    x = np.ascontiguousarray(inputs["x"], dtype=np.float32)
    c = np.ascontiguousarray(inputs["c"], dtype=np.float32)
    shared = {
        "w_ada": inputs["w_ada"][0], "b_ada": inputs["b_ada"][0], "norm_mix_g": inputs["norm_mix_g"][0],
        "w_in": inputs["w_in"][0], "ln_v_g": inputs["ln_v_g"][0], "ln_v_b": inputs["ln_v_b"][0],
        "w_spatial": inputs["w_spatial"][0], "b_spatial": inputs["b_spatial"][0],
        "w_proj_a": inputs["w_proj_a"][0], "w_proj_b": inputs["w_proj_b"][0], "w_out": inputs["w_out"][0],
        "norm_ffn_g": inputs["norm_ffn_g"][0], "w_ffn_gate": inputs["w_ffn_gate"][0],
        "w_ffn_up": inputs["w_ffn_up"][0], "w_ffn_down": inputs["w_ffn_down"][0],
        "norm_final_g": inputs["norm_final_g"],
    }
    shared = {k: np.ascontiguousarray(np.asarray(v), dtype=np.float32) for k, v in shared.items()}
    in_maps = []
    for core in range(8):
        b, half = core // 2, core % 2
        gm = np.full((16, 16), NEG, dtype=np.float32)
        for qt in range(16):
            o = qt // 2
            for blk in range(16):
                if blk < 8:
                    ok = (blk < o) or (blk == o and half == 1)
                else:
                    ok = (blk - 8) < o
                if ok:
                    gm[qt, blk] = 0.0
        vm = (gm == 0.0).astype(np.float32)
        m = dict(shared)
        for name in PADDED:
            w = shared[name]
            wp = np.empty((w.shape[0] + 1, w.shape[1]), dtype=np.float32)
            wp[:-1] = w
            wp[-1] = float(core)
            m[name] = wp
        xb = x[b].reshape(16, 256, D)
        m["x_own"] = np.ascontiguousarray(xb[half::2].reshape(NT, D))
        m["x_ctx"] = np.ascontiguousarray(xb[1 - half::2].reshape(NT, D))
        m["c_row"] = np.ascontiguousarray(c[b])
        m["gmask"] = np.ascontiguousarray(np.broadcast_to(gm.reshape(1, 256), (128, 256)))
        m["vmask"] = np.ascontiguousarray(np.broadcast_to(vm.reshape(1, 256), (128, 256)))
        in_maps.append(m)
    return in_maps


PADDED = ("w_ada", "w_in", "w_proj_a", "w_proj_b", "w_out", "w_ffn_gate", "w_ffn_up", "w_ffn_down")
_NC_CACHE = {}


def kernel(**inputs):
    if "nc" not in _NC_CACHE:
        _NC_CACHE["nc"] = build_program()[0]
    nc = _NC_CACHE["nc"]
    in_maps = make_in_maps(inputs)
    res = run_bass_kernel_spmd(nc, in_maps, core_ids=list(range(8)))
    out = np.empty((4, 4096, D), dtype=np.float32)
    for core in range(8):
        b, half = core // 2, core % 2
        out[b].reshape(16, 256, D)[half::2] = np.asarray(res.results[core]["out"], dtype=np.float32).reshape(8, 256, D)
    return out
```
